# Optimizing a Trainium2 kernel written in Bass

```python
import math
import jax, jax.numpy as jnp
from jax import lax
import numpy as np

D_MODEL = 2048
BATCH = 16
SEQ = 2048
DEPTH = 2

CTX_LEN = 256
GRID_W = 64
BLK = 128
EPS = 1e-6
N_MOD = 6

D_A = 2048
H_A = 8
DH_A = D_A // H_A

D_INNER = 2048
HEAD_P = 64
H_B = D_INNER // HEAD_P
G_B = 4
R_B = H_B // G_B
N_STATE = 128
CONV_W = 5
CONV_DIM = D_INNER + 2 * G_B * N_STATE

HEAD_DIM = 128
H_C = 8
KV_C = 2
H_D = 8
KV_D = 2
WINDOW = 128
ROPE_BASE = 10000.0
AXIS_DIM = HEAD_DIM // 2

D_FF = 4 * D_MODEL

AB_SPLITS = (2 * D_A, 2 * D_A + D_INNER, 2 * D_A + D_INNER + CONV_DIM)
AB_IN = 2 * D_A + D_INNER + CONV_DIM + 2 * H_B
AB_OUT_IN = D_A + D_INNER

QC_END = H_C * HEAD_DIM
KC_END = QC_END + KV_C * HEAD_DIM
VC_END = KC_END + KV_C * HEAD_DIM
QD_END = VC_END + H_D * HEAD_DIM
KD_END = QD_END + KV_D * HEAD_DIM
CD_IN = KD_END + KV_D * HEAD_DIM
CD_OUT_IN = (H_C + H_D) * HEAD_DIM

kernel_name = 'hybrid_dit_gmlp_ssd_axialgqa_swa'


def rmsnorm(x, g):
    x32 = x.astype(jnp.float32)
    y = x32 * lax.rsqrt(jnp.mean(x32 * x32, axis=-1, keepdims=True) + EPS)
    return (y * g.astype(jnp.float32)).astype(x.dtype)


def layernorm(x, g, b):
    x32 = x.astype(jnp.float32)
    mu = jnp.mean(x32, axis=-1, keepdims=True)
    xc = x32 - mu
    y = xc * lax.rsqrt(jnp.mean(xc * xc, axis=-1, keepdims=True) + EPS)
    return (y * g.astype(jnp.float32) + b.astype(jnp.float32)).astype(x.dtype)


def modulate(h, g, shift, scale):
    return rmsnorm(h, g) * (1 + scale) + shift


def mlp_sublayer(h, g_pre, g_post, shift, scale, gate, w1, w2):
    a = modulate(h, g_pre, shift, scale)
    y = jnp.square(jax.nn.relu(a @ w1)) @ w2
    return h + gate * rmsnorm(y, g_post)


def chunk_gmlp(p, w_s, b_s, ln_g, ln_b):
    bsz, L, _ = p.shape
    z = jax.nn.gelu(p, approximate=False)
    u, v = jnp.split(z, 2, axis=-1)
    v = layernorm(v, ln_g, ln_b).reshape(bsz, L // BLK, BLK, H_A, DH_A)
    v = jnp.einsum('hpq,bnqhd->bnphd', w_s, v) + b_s.T[:, :, None]
    return u * v.reshape(bsz, L, D_A)


def depthwise_conv_centred(x, w, b):
    y = lax.conv_general_dilated(
        x, w[:, None, :].astype(x.dtype), window_strides=(1,),
        padding=[(CONV_W // 2, CONV_W // 2)],
        dimension_numbers=('NWC', 'WIO', 'NWC'), feature_group_count=x.shape[-1])
    return y + b


def ssd_scan(xs, dt, a, bs, cs, h0):
    f32 = jnp.float32
    bsz, L = xs.shape[:2]
    nc = L // BLK
    x = xs.astype(f32).reshape(bsz, nc, BLK, G_B, R_B, HEAD_P)
    dtc = dt.astype(f32).reshape(bsz, nc, BLK, G_B, R_B)
    b = bs.astype(f32).reshape(bsz, nc, BLK, G_B, N_STATE)
    c = cs.astype(f32).reshape(bsz, nc, BLK, G_B, N_STATE)
    xdt = x * dtc[..., None]
    cum = jnp.cumsum(jnp.moveaxis(dtc * a, 2, -1), axis=-1)
    seg = cum[..., :, None] - cum[..., None, :]
    lower = jnp.tril(jnp.ones((BLK, BLK), dtype=bool))
    decay = jnp.where(lower, jnp.exp(jnp.where(lower, seg, 0.0)), 0.0)
    cb = jnp.einsum('bcign,bcjgn->bcgij', c, b)
    y_diag = jnp.einsum('bcgij,bcgrij,bcjgrp->bcigrp', cb, decay, xdt)
    to_end = jnp.exp(cum[..., -1:] - cum)
    states = jnp.einsum('bcjgn,bcgrj,bcjgrp->bcgrpn', b, to_end, xdt)
    chunk_decay = jnp.exp(cum[..., -1])

    def step(h, inp):
        s, d = inp
        return h * d[..., None, None] + s, h

    h_last, h_in = lax.scan(step, h0.astype(f32),
                            (jnp.moveaxis(states, 1, 0), jnp.moveaxis(chunk_decay, 1, 0)))
    h_in = jnp.moveaxis(h_in, 0, 1)
    y_off = jnp.einsum('bcign,bcgrpn,bcgri->bcigrp', c, h_in, jnp.exp(cum))
    y = (y_diag + y_off).reshape(xs.shape)
    return y.astype(xs.dtype), h_last


def ssd_branch(z, xbc, dt_raw, conv_w, conv_b, a_log, dt_bias, d_skip, gn_g, h0):
    bsz, L, _ = z.shape
    xbc = jax.nn.silu(depthwise_conv_centred(xbc, conv_w, conv_b))
    xs, bs, cs = jnp.split(xbc, (D_INNER, D_INNER + G_B * N_STATE), axis=-1)
    xs = xs.reshape(bsz, L, G_B, R_B, HEAD_P)
    bs = bs.reshape(bsz, L, G_B, N_STATE)
    cs = cs.reshape(bsz, L, G_B, N_STATE)
    dt = jax.nn.softplus(dt_raw.astype(jnp.float32).reshape(bsz, L, 2, G_B, R_B)
                         + dt_bias.astype(jnp.float32).reshape(2, G_B, R_B))
    a = -jnp.exp(a_log.astype(jnp.float32)).reshape(2, G_B, R_B)
    if h0 is None:
        zero = jnp.zeros((bsz, G_B, R_B, HEAD_P, N_STATE), jnp.float32)
        h0 = (zero, zero)
    flip = lambda t: jnp.flip(t, axis=1)
    y_f, h_f = ssd_scan(xs, dt[:, :, 0], a[0], bs, cs, h0[0])
    y_b, h_b = ssd_scan(flip(xs), flip(dt[:, :, 1]), a[1], flip(bs), flip(cs), h0[1])
    y = y_f + flip(y_b) + d_skip.reshape(G_B, R_B)[..., None] * xs
    y = y.reshape(bsz, L, D_INNER) * jax.nn.silu(z)
    y = rmsnorm(y.reshape(bsz, L, G_B, D_INNER // G_B), gn_g.reshape(G_B, -1))
    return y.reshape(bsz, L, D_INNER), (h_f, h_b)


def ab_mixer(hx, hc, w_in, w_s, b_s, ln_g, ln_b, conv_w, conv_b, a_log, dt_bias, d_skip, gn_g,
             w_out, need_ctx):
    pa_x, z_x, xbc_x, dt_x = jnp.split(hx @ w_in, AB_SPLITS, axis=-1)
    pa_c, z_c, xbc_c, dt_c = jnp.split(hc @ w_in, AB_SPLITS, axis=-1)
    yb_c, st_c = ssd_branch(z_c, xbc_c, dt_c, conv_w, conv_b, a_log, dt_bias, d_skip, gn_g, None)
    yb_x, _ = ssd_branch(z_x, xbc_x, dt_x, conv_w, conv_b, a_log, dt_bias, d_skip, gn_g, st_c)
    ya_x = chunk_gmlp(pa_x, w_s, b_s, ln_g, ln_b)
    y_x = jnp.concatenate([ya_x, yb_x], axis=-1) @ w_out
    y_c = None
    if need_ctx:
        ya_c = chunk_gmlp(pa_c, w_s, b_s, ln_g, ln_b)
        y_c = jnp.concatenate([ya_c, yb_c], axis=-1) @ w_out
    return y_x, y_c


def axial_rope(L):
    rows = L // GRID_W
    r_idx = jnp.repeat(jnp.arange(rows), GRID_W)
    c_idx = jnp.tile(jnp.arange(GRID_W), rows)
    pos = jnp.stack([r_idx, c_idx], axis=-1).astype(jnp.float32)
    inv_freq = ROPE_BASE ** (-jnp.arange(0, AXIS_DIM, 2, dtype=jnp.float32) / AXIS_DIM)
    ang = pos[:, :, None] * inv_freq
    return jnp.cos(ang), jnp.sin(ang)


def apply_rope(x, cos, sin):
    shp = x.shape
    xr = x.astype(jnp.float32).reshape(shp[:-1] + (2, 2, AXIS_DIM // 2))
    x1, x2 = xr[..., 0, :], xr[..., 1, :]
    bshape = (shp[1],) + (1,) * (x.ndim - 3) + (2, AXIS_DIM // 2)
    c = cos.reshape(bshape)
    s = sin.reshape(bshape)
    out = jnp.stack([x1 * c - x2 * s, x2 * c + x1 * s], axis=-2)
    return out.reshape(shp).astype(x.dtype)


def block_attn(q, k, v):
    bsz, L = q.shape[:2]
    scale = HEAD_DIM ** -0.5
    qb = jnp.moveaxis(q.reshape((bsz, L // BLK, BLK) + q.shape[2:]), 1, 0)

    def one(qblk):
        s = jnp.einsum('bqhrd,bkhd->bhrqk', qblk, k).astype(jnp.float32) * scale
        p = jax.nn.softmax(s, axis=-1).astype(v.dtype)
        return jnp.einsum('bhrqk,bkhd->bqhrd', p, v)

    o = lax.map(one, qb)
    return jnp.moveaxis(o, 0, 1).reshape(q.shape)


def window_sink_attn(q, k, v, k_ctx, v_ctx, sink):
    bsz, L = q.shape[:2]
    nb = L // BLK
    scale = HEAD_DIM ** -0.5

    def band(t):
        tb = jnp.pad(t, ((0, 0), (BLK, BLK), (0, 0), (0, 0))).reshape(bsz, nb + 2, BLK, t.shape[2], t.shape[3])
        return jnp.concatenate([tb[:, :-2], tb[:, 1:-1], tb[:, 2:]], axis=2)

    kb, vb = band(k), band(v)
    qb = q.reshape((bsz, nb, BLK) + q.shape[2:])
    s_loc = jnp.einsum('bnqhrd,bnkhd->bnhrqk', qb, kb).astype(jnp.float32) * scale
    blk = jnp.arange(nb)[:, None, None]
    qpos = blk * BLK + jnp.arange(BLK)[None, :, None]
    kpos = blk * BLK - BLK + jnp.arange(3 * BLK)[None, None, :]
    valid = (jnp.abs(kpos - qpos) <= WINDOW) & (kpos >= 0) & (kpos < L)
    s_loc = jnp.where(valid[None, :, None, None], s_loc, -jnp.inf)
    s_ctx = jnp.einsum('bnqhrd,bkhd->bnhrqk', qb, k_ctx).astype(jnp.float32) * scale
    s_sink = jnp.broadcast_to(sink.astype(jnp.float32)[None, None, :, :, None, None], s_loc.shape[:-1] + (1,))
    p = jax.nn.softmax(jnp.concatenate([s_loc, s_ctx, s_sink], axis=-1), axis=-1).astype(v.dtype)
    o = (jnp.einsum('bnhrqk,bnkhd->bnqhrd', p[..., :3 * BLK], vb)
         + jnp.einsum('bnhrqk,bkhd->bnqhrd', p[..., 3 * BLK:-1], v_ctx))
    return o.reshape(q.shape)


def sink_attn(q, k, v, sink):
    scale = HEAD_DIM ** -0.5
    s = jnp.einsum('bqhrd,bkhd->bhrqk', q, k).astype(jnp.float32) * scale
    s_sink = jnp.broadcast_to(sink.astype(jnp.float32)[None, :, :, None, None], s.shape[:-1] + (1,))
    p = jax.nn.softmax(jnp.concatenate([s, s_sink], axis=-1), axis=-1)[..., :-1].astype(v.dtype)
    return jnp.einsum('bhrqk,bkhd->bqhrd', p, v)


def cd_mixer(hx, hc, w_in, q_g, k_g, sink, w_out, need_ctx):
    bsz, L, _ = hx.shape
    Lc = hc.shape[1]
    qc, kc, vc, qd, kd, vd = jnp.split(hx @ w_in, (QC_END, KC_END, VC_END, QD_END, KD_END), axis=-1)
    cos, sin = axial_rope(L)
    qh = lambda t, n, kv: t.reshape(t.shape[0], t.shape[1], kv, n // kv, HEAD_DIM)
    kh = lambda t, kv: t.reshape(t.shape[0], t.shape[1], kv, HEAD_DIM)
    qc = apply_rope(rmsnorm(qh(qc, H_C, KV_C), q_g), cos, sin)
    kc = apply_rope(rmsnorm(kh(kc, KV_C), k_g), cos, sin)
    vc = kh(vc, KV_C)
    qd = apply_rope(qh(qd, H_D, KV_D), cos, sin)
    kd = apply_rope(kh(kd, KV_D), cos, sin)
    vd = kh(vd, KV_D)
    if need_ctx:
        qc_c, kc_c, vc_c, qd_c, kd_c, vd_c = jnp.split(hc @ w_in, (QC_END, KC_END, VC_END, QD_END, KD_END), axis=-1)
    else:
        kc_c, vc_c = jnp.split(hc @ w_in[:, QC_END:VC_END], 2, axis=-1)
        kd_c, vd_c = jnp.split(hc @ w_in[:, QD_END:], 2, axis=-1)
    kc_c = rmsnorm(kh(kc_c, KV_C), k_g)
    vc_c = kh(vc_c, KV_C)
    kd_c = kh(kd_c, KV_D)
    vd_c = kh(vd_c, KV_D)
    sink = sink.reshape(KV_D, H_D // KV_D)
    oc = block_attn(qc, jnp.concatenate([kc, kc_c], axis=1), jnp.concatenate([vc, vc_c], axis=1))
    od = window_sink_attn(qd, kd, vd, kd_c, vd_c, sink)
    y_x = jnp.concatenate([oc.reshape(bsz, L, -1), od.reshape(bsz, L, -1)], axis=-1) @ w_out
    y_c = None
    if need_ctx:
        qc_c = rmsnorm(qh(qc_c, H_C, KV_C), q_g)
        oc_c = block_attn(qc_c, kc_c, vc_c)
        od_c = sink_attn(qh(qd_c, H_D, KV_D), kd_c, vd_c, sink)
        y_c = jnp.concatenate([oc_c.reshape(bsz, Lc, -1), od_c.reshape(bsz, Lc, -1)], axis=-1) @ w_out
    return y_x, y_c


def setup_inputs(seed: int = 0) -> dict:
    key = jax.random.key(seed)
    ks = jax.random.split(key, 32)
    f32 = jnp.float32
    n_even = (DEPTH + 1) // 2
    n_odd = DEPTH // 2

    def nrm(k, shape, scale):
        return scale * jax.random.normal(k, shape, f32)

    dt0 = jnp.exp(jax.random.uniform(ks[15], (n_even, 2, H_B), f32, math.log(1e-3), math.log(1e-1)))
    return {
        'x': nrm(ks[0], (BATCH, SEQ, D_MODEL), 1.0),
        'c': nrm(ks[1], (BATCH, D_MODEL), 1.0),
        'ctx': nrm(ks[2], (BATCH, CTX_LEN, D_MODEL), 1.0),
        'c_ctx': nrm(ks[3], (D_MODEL,), 1.0),
        'mod_w': nrm(ks[4], (DEPTH, D_MODEL, N_MOD * D_MODEL), D_MODEL ** -0.5),
        'mod_b': nrm(ks[5], (DEPTH, N_MOD * D_MODEL), 0.02),
        'norm_g': 1.0 + nrm(ks[6], (DEPTH, 4, D_MODEL), 0.05),
        'mlp_w1': nrm(ks[7], (DEPTH, D_MODEL, D_FF), D_MODEL ** -0.5),
        'mlp_w2': nrm(ks[8], (DEPTH, D_FF, D_MODEL), D_FF ** -0.5),
        'ab_w_in': nrm(ks[9], (n_even, D_MODEL, AB_IN), D_MODEL ** -0.5),
        'a_w_s': nrm(ks[10], (n_even, H_A, BLK, BLK), BLK ** -0.5),
        'a_b_s': 1.0 + nrm(ks[11], (n_even, H_A, BLK), 0.1),
        'a_ln_g': 1.0 + nrm(ks[12], (n_even, D_A), 0.05),
        'a_ln_b': nrm(ks[13], (n_even, D_A), 0.02),
        'b_conv_w': nrm(ks[14], (n_even, CONV_W, CONV_DIM), CONV_W ** -0.5),
        'b_conv_b': nrm(ks[16], (n_even, CONV_DIM), 0.02),
        'b_a_log': jnp.log(jax.random.uniform(ks[17], (n_even, 2, H_B), f32, 1.0, 16.0)),
        'b_dt_bias': dt0 + jnp.log(-jnp.expm1(-dt0)),
        'b_d': 1.0 + nrm(ks[18], (n_even, H_B), 0.1),
        'b_norm_g': 1.0 + nrm(ks[19], (n_even, D_INNER), 0.05),
        'ab_w_out': nrm(ks[20], (n_even, AB_OUT_IN, D_MODEL), AB_OUT_IN ** -0.5),
        'cd_w_in': nrm(ks[21], (n_odd, D_MODEL, CD_IN), D_MODEL ** -0.5),
        'c_q_norm_g': 1.0 + nrm(ks[22], (n_odd, HEAD_DIM), 0.05),
        'c_k_norm_g': 1.0 + nrm(ks[23], (n_odd, HEAD_DIM), 0.05),
        'd_sink': nrm(ks[24], (n_odd, H_D), 0.5),
        'cd_w_out': nrm(ks[25], (n_odd, CD_OUT_IN, D_MODEL), CD_OUT_IN ** -0.5),
    }


def reference(x, c, ctx, c_ctx, mod_w, mod_b, norm_g, mlp_w1, mlp_w2, ab_w_in, a_w_s, a_b_s,
              a_ln_g, a_ln_b, b_conv_w, b_conv_b, b_a_log, b_dt_bias, b_d, b_norm_g, ab_w_out,
              cd_w_in, c_q_norm_g, c_k_norm_g, d_sink, cd_w_out):
    h_x, h_c = x, ctx
    s_lat = jax.nn.silu(c)
    s_ctx = jax.nn.silu(c_ctx)
    for i in range(DEPTH):
        last = i == DEPTH - 1
        j = i // 2
        sh1, sc1, g1, sh2, sc2, g2 = jnp.split((s_lat @ mod_w[i] + mod_b[i])[:, None, :], N_MOD, axis=-1)
        n_mc = 2 if last else N_MOD
        mc = jnp.split(s_ctx @ mod_w[i][:, :n_mc * D_MODEL] + mod_b[i][:n_mc * D_MODEL], n_mc, axis=-1)
        a_x = modulate(h_x, norm_g[i, 0], sh1, sc1)
        a_c = modulate(h_c, norm_g[i, 0], mc[0], mc[1])
        if i % 2 == 0:
            y_x, y_c = ab_mixer(a_x, a_c, ab_w_in[j], a_w_s[j], a_b_s[j], a_ln_g[j], a_ln_b[j],
                                b_conv_w[j], b_conv_b[j], b_a_log[j], b_dt_bias[j], b_d[j],
                                b_norm_g[j], ab_w_out[j], not last)
        else:
            y_x, y_c = cd_mixer(a_x, a_c, cd_w_in[j], c_q_norm_g[j], c_k_norm_g[j], d_sink[j],
                                cd_w_out[j], not last)
        h_x = h_x + g1 * rmsnorm(y_x, norm_g[i, 1])
        h_x = mlp_sublayer(h_x, norm_g[i, 2], norm_g[i, 3], sh2, sc2, g2, mlp_w1[i], mlp_w2[i])
        if not last:
            h_c = h_c + mc[2] * rmsnorm(y_c, norm_g[i, 1])
            h_c = mlp_sublayer(h_c, norm_g[i, 2], norm_g[i, 3], mc[3], mc[4], mc[5], mlp_w1[i], mlp_w2[i])
    return h_x
```

```python
import numpy as np
from contextlib import ExitStack
import concourse.bass as bass
import concourse.mybir as mybir
from concourse.bass_utils import run_bass_kernel_spmd

F32 = mybir.dt.float32
BF16 = mybir.dt.bfloat16
AF = mybir.ActivationFunctionType
ALU = mybir.AluOpType

NCORES = 8
D = 2048
DFF = 8192
SEQT = 2304
CTXL = 256
LAT = 2048
NT = 2 * SEQT
EPS = 1e-6
AB_IN = 9280
NH_B = 32


class Buf:
    __slots__ = ("t", "w", "r", "name")

    def __init__(self, t, name=None):
        self.t = t
        self.w = {}
        self.r = {}
        self.name = name

    def __getitem__(self, k):
        return self.t[k]


class Ctx:
    def __init__(self, nc, n_dma_sems=56):
        self.nc = nc
        self.E = {"pe": nc.tensor, "act": nc.scalar, "dve": nc.vector, "pool": nc.gpsimd, "sp": nc.sync}
        self.sem = {e: nc.alloc_semaphore("c_" + e) for e in ("pe", "act", "dve", "pool")}
        self.cnt = {e: 0 for e in self.sem}
        self.pend = {e: False for e in self.sem}
        self.waited = {e: {} for e in self.E}
        self.dsem = [nc.alloc_semaphore("d%d" % i) for i in range(n_dma_sems)]
        self.dval = [0] * n_dma_sems
        self.dpool = {"sp": list(range(0, 36)), "pool": list(range(36, n_dma_sems))}
        self.dnext = {"sp": 0, "pool": 0}
        self.bar = nc.alloc_semaphore("bar")
        self.barv = 0
        self.uid = 0
        self.stack = None

    def sb(self, shape, dtype, name="t"):
        self.uid += 1
        t = self.stack.enter_context(self.nc.sbuf_tensor("%s_%d" % (name, self.uid), list(shape), dtype))
        return Buf(t, name)

    def ps(self, shape=(128, 512), dtype=F32, name="p"):
        self.uid += 1
        t = self.stack.enter_context(self.nc.psum_tensor("%s_%d" % (name, self.uid), list(shape), dtype))
        return Buf(t, name)

    def _wait(self, eng, key, val):
        if key == ("c", "pe") and eng == "pe":
            return
        if self.waited[eng].get(key, 0) >= val:
            return
        sem = self.sem[key[1]] if key[0] == "c" else self.dsem[key[1]]
        self.E[eng].wait_ge(sem, val)
        self.waited[eng][key] = val

    def _deps(self, eng, reads, writes, join):
        for b in reads:
            for k, v in b.w.items():
                self._wait(eng, k, v)
        for b in writes:
            if not join:
                for k, v in b.w.items():
                    self._wait(eng, k, v)
            for k, v in b.r.items():
                self._wait(eng, k, v)

    def _note(self, key, val, reads, writes, join):
        for b in reads:
            if b.r.get(key, 0) < val:
                b.r[key] = val
        for b in writes:
            if join:
                b.w[key] = val
            else:
                b.w = {key: val}
                b.r = {}

    def op(self, eng, fn, reads=(), writes=(), inc=True, join=False):
        reads = [b for b in reads if b is not None]
        writes = [b for b in writes if b is not None]
        self._deps(eng, reads, writes, join)
        ins = fn(self.E[eng])
        if eng != "pe":
            inc = True
        if inc:
            self.cnt[eng] += 1
            ins.then_inc(self.sem[eng], 1)
            val = self.cnt[eng]
            self.pend[eng] = False
        else:
            val = self.cnt[eng] + 1
            self.pend[eng] = True
        self._note(("c", eng), val, reads, writes, join)
        return ins

    def dma(self, q, out_ap, in_ap, reads=(), writes=(), join=False, **kw):
        reads = [b for b in reads if b is not None]
        writes = [b for b in writes if b is not None]
        self._deps(q, reads, writes, join)
        pool = self.dpool[q]
        i = pool[self.dnext[q] % len(pool)]
        self.dnext[q] += 1
        if self.dval[i] > 0:
            self._wait(q, ("d", i), self.dval[i])
        self.dval[i] += 16
        ins = self.E[q].dma_start(out=out_ap, in_=in_ap, **kw)
        ins.then_inc(self.dsem[i], 16)
        self._note(("d", i), self.dval[i], reads, writes, join)
        return ins

    def barrier(self):
        assert not self.pend["pe"], "PE has an un-flushed milestone"
        for e in self.sem:
            if self.cnt[e] > 0:
                self._wait("sp", ("c", e), self.cnt[e])
        for i, v in enumerate(self.dval):
            if v > 0:
                self._wait("sp", ("d", i), v)
        self.barv += 1
        self.E["sp"].sem_inc(self.bar, 1)
        for e in ("pe", "act", "dve", "pool"):
            self.E[e].wait_ge(self.bar, self.barv)
            for e2 in self.sem:
                self.waited[e][("c", e2)] = self.cnt[e2]
            for i, v in enumerate(self.dval):
                self.waited[e][("d", i)] = v


def subs(W):
    n = (W + 511) // 512
    assert W % n == 0
    w = W // n
    return [(i * w, w) for i in range(n)]


def tiles_all():
    return [(s * SEQT + j * 768, 768) for s in range(2) for j in range(3)]


def tiles_lat():
    out = []
    for s in range(2):
        b = s * SEQT + CTXL
        out += [(b, 768), (b + 768, 768), (b + 1536, 512)]
    return out


def tile_segs(c0, W):
    s = c0 // SEQT
    off = c0 - s * SEQT
    if off < CTXL:
        return [(0, CTXL - off, 2), (CTXL - off, W, s)]
    return [(0, W, s)]


class Prog:
    def __init__(self, dbg=(), stop_after=None, layers=(0, 1)):
        self.nc = nc = bass.Bass("TRN2", target_bir_lowering=False)
        self.c = Ctx(nc)
        self.dbg = set(dbg)
        self.stop_after = stop_after
        self.layers = layers
        self.I = {}
        self.done = False
        self.wdone = {}

    def inp(self, name, shape, dt=F32):
        self.I[name] = self.nc.dram_tensor(name, list(shape), dt, kind="ExternalInput")
        return self.I[name]

    def dram(self, name, shape, dt):
        kind = "ExternalOutput" if name in self.dbg else "Internal"
        return self.nc.dram_tensor(name, list(shape), dt, kind=kind)

    def declare(self):
        inp = self.inp
        inp("hT", [D, NT]); inp("cT", [128, 16, 3])
        inp("mod_w", [2, D, 6 * D]); inp("mod_b", [128, 2, 96]); inp("norm_g", [128, 2, 4, 16])
        inp("mlp_w1", [2, D, DFF]); inp("mlp_w2", [2, DFF, D])
        inp("ab_w_in", [D, AB_IN]); inp("ab_w_out", [2 * D, D]); inp("cd_w_in", [D, 3072]); inp("cd_w_out", [D, D])
        inp("wsT", [128, 8, 128]); inp("bs_row", [1, 2048]); inp("lng_row", [1, 2048]); inp("lnb_row", [1, 2048])
        inp("conv_w", [128, 24, 5]); inp("conv_b", [128, 24])
        inp("alog_row", [1, 64]); inp("dtb_row", [1, 64]); inp("bd", [128, 16]); inp("gng", [128, 16])
        inp("qg", [128, 1]); inp("kg", [128, 1]); inp("sink_row", [1, 8])
        inp("ident", [128, 128]); inp("tri_f", [128, 128]); inp("tri_b", [128, 128]); inp("rotT", [128, 128])
        inp("cosE", [128, SEQT]); inp("sinE", [128, SEQT]); inp("wmask", [128, 3, 128])
        d = self.dram
        self.HA = d("HA", [D, NT], F32)
        self.HB = d("HB", [D, NT], F32)
        self.OUT = self.nc.dram_tensor("outT", [D, 2 * LAT], F32, kind="ExternalOutput")
        self.U = d("U", [D, NT], BF16)
        self.V = d("V", [NT, D], BF16)
        self.SZ = d("SZ", [D, NT], BF16)
        self.XBC = d("XBC", [3072, NT], F32)
        self.XC = d("XC", [3072, NT], F32)
        self.DTR = d("DTR", [NT, 64], F32)
        self.YF = d("YF", [D, NT], F32)
        self.YAB = d("YAB", [2 * D, NT], BF16)
        self.QKT = d("QKT", [20 * 128, NT], BF16)
        self.VT = d("VT", [NT, 512], BF16)
        self.W1B = d("W1B", [2, D, DFF], BF16); self.W2B = d("W2B", [2, DFF, D], BF16)
        self.ABIB = d("ABIB", [D, AB_IN], BF16); self.ABOB = d("ABOB", [2 * D, D], BF16)
        self.CDIB = d("CDIB", [D, 3072], BF16); self.CDOB = d("CDOB", [D, D], BF16)

    def persistent(self):
        c = self.c
        self.modT = c.sb([128, 2, 96, 3], F32, "modT")
        self.gsc = c.sb([128, 2, 2, 16, 3], F32, "gsc")
        self.gg = c.sb([128, 2, 2, 16, 3], F32, "gg")
        self.ng = c.sb([128, 2, 4, 16], F32, "ng")
        self.ones_bf = c.sb([128, 128], BF16, "ones")
        self.ones_f = c.sb([128, 128], F32, "onesf")
        self.eps_t = c.sb([128, 1], F32, "eps")
        c.op("dve", lambda e: e.memset(self.ones_bf[:], 1.0), writes=[self.ones_bf])
        c.op("dve", lambda e: e.memset(self.ones_f[:], 1.0), writes=[self.ones_f])
        c.op("dve", lambda e: e.memset(self.eps_t[:], EPS), writes=[self.eps_t])
        c.dma("sp", self.ng[:], self.I["norm_g"][:, :, :, :], writes=[self.ng])

    def end_phase(self, name):
        self.c.barrier()
        if self.stop_after == name:
            self.done = True
        return self.done

    def phase_mod(self):
        c, I = self.c, self.I
        with ExitStack() as st:
            c.stack = st
            cT = c.sb([128, 16, 3], F32); sbf = c.sb([128, 16, 3], BF16); modb = c.sb([128, 2, 96], F32)
            c.dma("sp", cT[:], I["cT"][:, :, :], writes=[cT])
            c.dma("sp", modb[:], I["mod_b"][:, :, :], writes=[modb])
            c.op("act", lambda e: e.activation(out=sbf[:], in_=cT[:], func=AF.Silu), reads=[cT], writes=[sbf])
            wbuf = [c.sb([128, 16, 1536], BF16, "modw") for _ in range(2)]
            ps = [c.ps() for _ in range(2)]
            it = 0
            for i in range(2):
                for cb in range(8):
                    wb = wbuf[it % 2]; p = ps[it % 2]; it += 1
                    for q in range(4):
                        c.dma("pool", wb[:, 4 * q:4 * q + 4, :],
                              I["mod_w"][i, 512 * q:512 * (q + 1), cb * 1536:(cb + 1) * 1536].rearrange("(kc p) n -> p kc n", p=128),
                              writes=[wb], join=(q > 0))
                    for oc in range(12):
                        for kc in range(16):
                            c.op("pe", lambda e: e.matmul(p[:, oc * 3:oc * 3 + 3], wb[:, kc, oc * 128:(oc + 1) * 128], sbf[:, kc, :],
                                                          start=(kc == 0), stop=(kc == 15)),
                                 reads=[wb, sbf], writes=[p], inc=(oc == 11 and kc == 15))
                    c.op("dve", lambda e: e.tensor_tensor(out=self.modT[:, i, cb * 12:(cb + 1) * 12, :],
                                                          in0=p[:, 0:36].rearrange("p (a b) -> p a b", b=3),
                                                          in1=modb[:, i, cb * 12:(cb + 1) * 12].unsqueeze(2).to_broadcast([128, 12, 3]),
                                                          op=ALU.add),
                         reads=[p, modb], writes=[self.modT], join=True)
            m, ng = self.modT, self.ng
            for i in range(2):
                for (lo, which, gi, dst) in ((16, 0, 0, self.gsc), (64, 1, 2, self.gsc)):
                    c.op("dve", lambda e: e.scalar_tensor_tensor(out=dst[:, i, which, :, :], in0=m[:, i, lo:lo + 16, :], scalar=1.0,
                                                                 in1=ng[:, i, gi, :].unsqueeze(2).to_broadcast([128, 16, 3]),
                                                                 op0=ALU.add, op1=ALU.mult),
                         reads=[m, ng], writes=[dst], join=True)
                for (lo, which, gi, dst) in ((32, 0, 1, self.gg), (80, 1, 3, self.gg)):
                    c.op("dve", lambda e: e.tensor_tensor(out=dst[:, i, which, :, :], in0=m[:, i, lo:lo + 16, :],
                                                          in1=ng[:, i, gi, :].unsqueeze(2).to_broadcast([128, 16, 3]), op=ALU.mult),
                         reads=[m, ng], writes=[dst], join=True)

    def shift_ap(self, i, which, kc, s):
        lo = 0 if which == 0 else 48
        return self.modT[:, i, lo + kc, s:s + 1]

    def norm_scratch(self, W):
        c = self.c
        S = {}
        S["hb"] = [c.sb([128, W], F32, "hb") for _ in range(3)]
        S["sq"] = [c.sb([128, W], BF16, "sq") for _ in range(2)]
        S["rstd"] = c.sb([128, W], F32, "rstd")
        S["tn"] = [c.sb([128, W], F32, "tn") for _ in range(2)]
        S["ps"] = [c.ps() for _ in range(2)]
        return S

    def rstd_from_ps(self, S, W, n):
        c = self.c
        for si, (lo, w) in enumerate(subs(W)):
            p = S["ps"][si]
            c.op("act", lambda e: e.activation(out=S["rstd"][:, lo:lo + w], in_=p[:, 0:w], func=AF.Sqrt,
                                               scale=1.0 / n, bias=self.eps_t[:, 0:1]),
                 reads=[p, self.eps_t], writes=[S["rstd"]], join=(si > 0))
        c.op("dve", lambda e: e.reciprocal(out=S["rstd"][:, 0:W], in_=S["rstd"][:, 0:W]), reads=[S["rstd"]], writes=[S["rstd"]])

    def load_modnorm(self, S, hsrc, c0, W, layer, which, aT):
        for _ in self.gen_modnorm(S, hsrc, c0, W, layer, which, aT):
            pass

    def gen_modnorm(self, S, hsrc, c0, W, layer, which, aT):
        c = self.c
        sb = subs(W)
        for kc in range(16):
            hb = S["hb"][kc % 3]
            c.dma("sp", hb[:, 0:W], hsrc[kc * 128:(kc + 1) * 128, c0:c0 + W], writes=[hb])
            sq = S["sq"][kc % 2]
            c.op("act", lambda e: e.activation(out=sq[:, 0:W], in_=hb[:, 0:W], func=AF.Square), reads=[hb], writes=[sq])
            for si, (lo, w) in enumerate(sb):
                p = S["ps"][si]
                c.op("pe", lambda e: e.matmul(p[:, 0:w], self.ones_bf[:, :], sq[:, lo:lo + w], start=(kc == 0), stop=(kc == 15)),
                     reads=[sq, self.ones_bf], writes=[p], inc=(si == len(sb) - 1))
            yield
        self.rstd_from_ps(S, W, float(D))
        yield
        segs = tile_segs(c0, W)
        for kc in range(16):
            hb = S["hb"][kc % 3]
            c.dma("sp", hb[:, 0:W], hsrc[kc * 128:(kc + 1) * 128, c0:c0 + W], writes=[hb])
            t = S["tn"][kc % 2]
            c.op("dve", lambda e: e.tensor_tensor(out=t[:, 0:W], in0=hb[:, 0:W], in1=S["rstd"][:, 0:W], op=ALU.mult),
                 reads=[hb, S["rstd"]], writes=[t])
            for gi, (lo, hi, s) in enumerate(segs):
                c.op("act", lambda e: e.activation(out=aT[:, kc, lo:hi], in_=t[:, lo:hi], func=AF.Identity,
                                                   scale=self.gsc[:, layer, which, kc, s:s + 1], bias=self.shift_ap(layer, which, kc, s)),
                     reads=[t, self.gsc, self.modT], writes=[aT], join=True)
            yield

    def load_w(self, wb, src2d, KC, ncols, cache=None, key=None, col0=0):
        c = self.c
        step = 4
        if cache is not None and key in self.wdone:
            trk = self.wdone[key]
            for k0 in range(0, KC, step):
                c.dma("pool", wb[:, k0:k0 + step, col0:col0 + ncols],
                      cache[k0 * 128:(k0 + step) * 128, :].rearrange("(kc p) n -> p kc n", p=128),
                      reads=[trk], writes=[wb], join=(k0 > 0 or col0 > 0))
            return
        for k0 in range(0, KC, step):
            c.dma("pool", wb[:, k0:k0 + step, col0:col0 + ncols],
                  src2d[k0 * 128:(k0 + step) * 128, :].rearrange("(kc p) n -> p kc n", p=128),
                  writes=[wb], join=(k0 > 0 or col0 > 0))
        if cache is not None:
            trk = Buf(None, "wtrk")
            for k0 in range(0, KC, step):
                c.dma("sp", cache[k0 * 128:(k0 + step) * 128, :].rearrange("(kc p) n -> p kc n", p=128),
                      wb[:, k0:k0 + step, col0:col0 + ncols], reads=[wb], writes=[trk], join=True)
            self.wdone[key] = trk

    def post_residual(self, S, yacc, W, hsrc, c0, layer, which, hdst, d0):
        for _ in self.gen_post_residual(S, yacc, W, hsrc, c0, layer, which, hdst, d0):
            pass

    def gen_post_residual(self, S, yacc, W, hsrc, c0, layer, which, hdst, d0):
        c = self.c
        sb = subs(W)
        for kc in range(16):
            sq = S["sq"][kc % 2]
            c.op("act", lambda e: e.activation(out=sq[:, 0:W], in_=yacc[:, kc, 0:W], func=AF.Square), reads=[yacc], writes=[sq])
            for si, (lo, w) in enumerate(sb):
                p = S["ps"][si]
                c.op("pe", lambda e: e.matmul(p[:, 0:w], self.ones_bf[:, :], sq[:, lo:lo + w], start=(kc == 0), stop=(kc == 15)),
                     reads=[sq, self.ones_bf], writes=[p], inc=(si == len(sb) - 1))
            yield
        self.rstd_from_ps(S, W, float(D))
        yield
        segs = tile_segs(c0, W)
        for kc in range(16):
            hb = S["hb"][kc % 3]
            c.dma("sp", hb[:, 0:W], hsrc[kc * 128:(kc + 1) * 128, c0:c0 + W], writes=[hb])
            t = S["tn"][kc % 2]
            c.op("dve", lambda e: e.tensor_tensor(out=t[:, 0:W], in0=yacc[:, kc, 0:W], in1=S["rstd"][:, 0:W], op=ALU.mult),
                 reads=[yacc, S["rstd"]], writes=[t])
            for (lo, hi, s) in segs:
                c.op("dve", lambda e: e.scalar_tensor_tensor(out=hb[:, lo:hi], in0=t[:, lo:hi], scalar=self.gg[:, layer, which, kc, s:s + 1],
                                                              in1=hb[:, lo:hi], op0=ALU.mult, op1=ALU.add),
                     reads=[t, self.gg, hb], writes=[hb], join=True)
            c.dma("sp", hdst[kc * 128:(kc + 1) * 128, d0:d0 + W], hb[:, 0:W], reads=[hb])
            yield

    def phase_mlp(self, layer, hsrc, hdst, tiles, dst_map):
        c, I = self.c, self.I
        w1 = I["mlp_w1"]; w2 = I["mlp_w2"]
        WT = 256
        NJ = 2048 // WT
        NO = WT // 128
        with ExitStack() as st:
            c.stack = st
            WMAX = 768
            S = self.norm_scratch(WMAX)
            aTs = [c.sb([128, 16, WMAX], BF16, "aT") for _ in range(2)]
            yacc = c.sb([128, 16, WMAX], F32, "yacc")
            h1 = c.sb([128, 16, WMAX], BF16, "h1")
            w1b = [c.sb([128, 16, WT], BF16, "w1b") for _ in range(2)]
            w2b = [c.sb([128, 16, WT], BF16, "w2b") for _ in range(2)]
            sqt = [c.sb([128, 512], F32, "sqt") for _ in range(2)]
            pp = [c.ps() for _ in range(6)]
            pi = 0
            self.load_modnorm(S, hsrc, tiles[0][0], tiles[0][1], layer, 1, aTs[0])
            pend_post = None
            for ti, (c0, W) in enumerate(tiles):
                aT = aTs[ti % 2]
                pend_norm = None
                if ti + 1 < len(tiles):
                    pend_norm = self.gen_modnorm(S, hsrc, tiles[ti + 1][0], tiles[ti + 1][1], layer, 1, aTs[(ti + 1) % 2])
                sb = subs(W)
                sched = []
                for fb in range(4):
                    sched += [("w1", fb, j) for j in range(NJ)] + [("w2", fb, j) for j in range(NJ)]

                def issue(idx):
                    kind, fb, j = sched[idx]
                    if kind == "w1":
                        self.load_w(w1b[j % 2], w1[layer, :, fb * 2048 + j * WT: fb * 2048 + (j + 1) * WT], 16, WT,
                                    cache=self.W1B[layer, :, fb * 2048 + j * WT: fb * 2048 + (j + 1) * WT], key=("w1", layer, fb, j))
                    else:
                        self.load_w(w2b[j % 2], w2[layer, fb * 2048:(fb + 1) * 2048, j * WT:(j + 1) * WT], 16, WT,
                                    cache=self.W2B[layer, fb * 2048:(fb + 1) * 2048, j * WT:(j + 1) * WT], key=("w2", layer, fb, j))

                issue(0)
                for idx, (kind, fb, j) in enumerate(sched):
                    if idx + 1 < len(sched):
                        issue(idx + 1)
                    if kind == "w2" and fb == 0 and j == 0 and pend_post is not None:
                        for _ in pend_post:
                            pass
                        pend_post = None
                    wb = (w1b if kind == "w1" else w2b)[j % 2]
                    for oc in range(NO):
                        och = j * NO + oc
                        for (lo, w) in sb:
                            if fb == 0 and kind == "w1" and pend_post is not None:
                                next(pend_post, None); next(pend_post, None)
                            elif fb >= 1 and pend_norm is not None:
                                next(pend_norm, None)
                            p = pp[pi % 6]; pi += 1
                            if kind == "w1":
                                for kc in range(16):
                                    c.op("pe", lambda e: e.matmul(p[:, 0:w], wb[:, kc, oc * 128:(oc + 1) * 128], aT[:, kc, lo:lo + w],
                                                                  start=(kc == 0), stop=(kc == 15)),
                                         reads=[wb, aT], writes=[p], inc=(kc == 15))
                                sq = sqt[pi % 2]
                                c.op("act", lambda e: e.activation(out=sq[:, 0:w], in_=p[:, 0:w], func=AF.Square), reads=[p], writes=[sq])
                                c.op("dve", lambda e: e.scalar_tensor_tensor(out=h1[:, och, lo:lo + w], in0=p[:, 0:w], scalar=0.0, in1=sq[:, 0:w],
                                                                             op0=ALU.is_gt, op1=ALU.mult),
                                     reads=[p, sq], writes=[h1], join=True)
                            else:
                                for kc in range(16):
                                    c.op("pe", lambda e: e.matmul(p[:, 0:w], wb[:, kc, oc * 128:(oc + 1) * 128], h1[:, kc, lo:lo + w],
                                                                  start=(kc == 0), stop=(kc == 15)),
                                         reads=[wb, h1], writes=[p], inc=(kc == 15))
                                if fb == 0:
                                    c.op("act", lambda e: e.activation(out=yacc[:, och, lo:lo + w], in_=p[:, 0:w], func=AF.Copy),
                                         reads=[p], writes=[yacc], join=True)
                                else:
                                    c.op("dve", lambda e: e.tensor_tensor(out=yacc[:, och, lo:lo + w], in0=p[:, 0:w], in1=yacc[:, och, lo:lo + w], op=ALU.add),
                                         reads=[p, yacc], writes=[yacc], join=True)
                if pend_norm is not None:
                    for _ in pend_norm:
                        pass
                if pend_post is not None:
                    for _ in pend_post:
                        pass
                pend_post = self.gen_post_residual(S, yacc, W, hsrc, c0, layer, 1, hdst, dst_map(c0))
            for _ in pend_post:
                pass

    def phase_outproj(self, layer, wout, KC, hsrc, hdst, tiles):
        c = self.c
        with ExitStack() as st:
            c.stack = st
            WMAX = 768
            S = self.norm_scratch(WMAX)
            src = c.sb([128, KC, WMAX], BF16, "src")
            yaccs = [c.sb([128, 16, WMAX], F32, "yacc") for _ in range(2)]
            wbs = [c.sb([128, KC, 256], BF16, "wo") for _ in range(2)]
            pp = [c.ps() for _ in range(6)]
            pi = 0
            pend_post = None
            wcache = self.ABOB if layer == 0 else self.CDOB
            for ti, (c0, W) in enumerate(tiles):
                yacc = yaccs[ti % 2]
                for k0 in range(0, KC, 4):
                    c.dma("sp", src[:, k0:k0 + 4, 0:W], self.YAB[k0 * 128:(k0 + 4) * 128, c0:c0 + W].rearrange("(kc p) n -> p kc n", p=128),
                          writes=[src], join=(k0 > 0))
                sb = subs(W)
                self.load_w(wbs[0], wout[:, 0:256], KC, 256, cache=wcache[:, 0:256], key=("wo", layer, 0))
                for j in range(8):
                    if j + 1 < 8:
                        self.load_w(wbs[(j + 1) % 2], wout[:, (j + 1) * 256:(j + 2) * 256], KC, 256,
                                    cache=wcache[:, (j + 1) * 256:(j + 2) * 256], key=("wo", layer, j + 1))
                    wb = wbs[j % 2]
                    for oc in range(2):
                        ob = j * 2 + oc
                        for (lo, w) in sb:
                            if pend_post is not None:
                                next(pend_post, None); next(pend_post, None)
                            p = pp[pi % 6]; pi += 1
                            for kc in range(KC):
                                c.op("pe", lambda e: e.matmul(p[:, 0:w], wb[:, kc, oc * 128:(oc + 1) * 128], src[:, kc, lo:lo + w],
                                                              start=(kc == 0), stop=(kc == KC - 1)),
                                     reads=[wb, src], writes=[p], inc=(kc == KC - 1))
                            c.op("act", lambda e: e.activation(out=yacc[:, ob, lo:lo + w], in_=p[:, 0:w], func=AF.Copy),
                                 reads=[p], writes=[yacc], join=True)
                if pend_post is not None:
                    for _ in pend_post:
                        pass
                pend_post = self.gen_post_residual(S, yacc, W, hsrc, c0, layer, 0, hdst, c0)
            for _ in pend_post:
                pass

    def phase_cd_inproj(self, hsrc):
        c, I = self.c, self.I
        wi = I["cd_w_in"]
        with ExitStack() as st:
            c.stack = st
            W = 768
            S = self.norm_scratch(W)
            aT = c.sb([128, 16, W], BF16, "aT")
            wbs = [c.sb([128, 16, 512], BF16, "wi") for _ in range(2)]
            cosE = c.sb([128, SEQT], F32, "cos"); sinE = c.sb([128, SEQT], F32, "sin")
            rotT = c.sb([128, 128], F32, "rot"); qg = c.sb([128, 1], F32); kg = c.sb([128, 1], F32)
            c.dma("sp", cosE[:], I["cosE"][:, :], writes=[cosE]); c.dma("sp", sinE[:], I["sinE"][:, :], writes=[sinE])
            c.dma("sp", rotT[:], I["rotT"][:, :], writes=[rotT])
            c.dma("sp", qg[:], I["qg"][:, :], writes=[qg]); c.dma("sp", kg[:], I["kg"][:, :], writes=[kg])
            xs = [c.sb([128, 384], F32, "xs") for _ in range(3)]
            xn = [c.sb([128, 384], F32, "xn") for _ in range(3)]
            sqb = [c.sb([128, 384], BF16, "sqb") for _ in range(3)]
            rs = [c.sb([128, 384], F32, "rs") for _ in range(3)]
            t1 = [c.sb([128, 384], F32, "t1") for _ in range(3)]
            t2 = [c.sb([128, 384], F32, "t2") for _ in range(3)]
            ob = [c.sb([128, W], BF16, "ob") for _ in range(2)]
            vob = [c.sb([128, 6, 256], BF16, "vob") for _ in range(2)]
            pg = [c.ps() for _ in range(3)]
            pn = c.ps(); pr = c.ps(); pv = c.ps()
            plan = []
            for t in range(6):
                if t in (0, 1):
                    plan.append([("C", qg, t * 4 + o) for o in range(4)])
                elif t == 2:
                    plan.append([("C", kg, 8), ("C", kg, 9), ("V", None, 0)])
                elif t in (3, 4):
                    plan.append([("D", None, 10 + (t - 3) * 4 + o) for o in range(4)])
                else:
                    plan.append([("D", None, 18), ("D", None, 19), ("V", None, 256)])
            ei = 0
            oc_i = [0]
            stage2 = [None]; stage3 = [None]
            for (c0, _) in tiles_all():
                self.load_modnorm(S, hsrc, c0, W, 1, 0, aT)
                sq0 = c0 - (c0 // SEQT) * SEQT
                sb = subs(W)
                self.load_w(wbs[0], wi[:, 0:512], 16, 512, cache=self.CDIB[:, 0:512], key=("cdi", 0))
                for t in range(6):
                    if t + 1 < 6:
                        self.load_w(wbs[(t + 1) % 2], wi[:, (t + 1) * 512:(t + 2) * 512], 16, 512,
                                    cache=self.CDIB[:, (t + 1) * 512:(t + 2) * 512], key=("cdi", t + 1))
                    wb = wbs[t % 2]
                    for oi, (kind, gv, row) in enumerate(plan[t]):
                        if kind == "V":
                            vo = vob[(t // 3) % 2]
                            for tb in range(6):
                                for kc in range(16):
                                    c.op("pe", lambda e: e.matmul(pv[:, 0:256], aT[:, kc, tb * 128:(tb + 1) * 128], wb[:, kc, 256:512],
                                                                  start=(kc == 0), stop=(kc == 15)),
                                         reads=[wb, aT], writes=[pv], inc=(kc == 15))
                                c.op("act", lambda e: e.activation(out=vo[:, tb, :], in_=pv[:, 0:256], func=AF.Copy), reads=[pv], writes=[vo], join=True)
                            c.dma("sp", self.VT[c0:c0 + W, row:row + 256].rearrange("(n p) d -> p n d", p=128), vo[:, :, :], reads=[vo])
                            continue
                        o = ob[oc_i[0] % 2]; oc_i[0] += 1
                        for si, (lo, w) in enumerate(sb):
                            k = ei % 3; ei += 1
                            p = pg[k]
                            for kc in range(16):
                                c.op("pe", lambda e: e.matmul(p[:, 0:w], wb[:, kc, oi * 128:(oi + 1) * 128], aT[:, kc, lo:lo + w],
                                                              start=(kc == 0), stop=(kc == 15)),
                                     reads=[wb, aT], writes=[p], inc=(kc == 15))
                            x = xs[k]
                            c.op("act", lambda e: e.activation(out=x[:, 0:w], in_=p[:, 0:w], func=AF.Copy), reads=[p], writes=[x])
                            if kind == "C":
                                c.op("act", lambda e: e.activation(out=sqb[k][:, 0:w], in_=x[:, 0:w], func=AF.Square), reads=[x], writes=[sqb[k]])
                            if stage3[0] is not None:
                                stage3[0](); stage3[0] = None
                            if stage2[0] is not None:
                                stage3[0] = stage2[0](); stage2[0] = None
                            stage2[0] = self._cd_stage2(kind, gv, k, w, lo, sq0, o, row, c0, W, si == len(sb) - 1,
                                                        xs, xn, sqb, rs, t1, t2, pn, pr, rotT, cosE, sinE)
                if stage2[0] is not None:
                    if stage3[0] is not None:
                        stage3[0](); stage3[0] = None
                    stage3[0] = stage2[0](); stage2[0] = None
                if stage3[0] is not None:
                    stage3[0](); stage3[0] = None

    def _cd_stage2(self, kind, gv, k, w, lo, sq0, o, row, c0, W, last, xs, xn, sqb, rs, t1, t2, pn, pr, rotT, cosE, sinE):
        c = self.c
        x = xs[k]

        def stage3(y):
            c.op("pe", lambda e: e.matmul(pr[:, 0:w], rotT[:, :], y[:, 0:w], start=True, stop=True), reads=[rotT, y], writes=[pr])
            cl = sq0 + lo
            c.op("pool", lambda e: e.tensor_tensor(out=t1[k][:, 0:w], in0=y[:, 0:w], in1=cosE[:, cl:cl + w], op=ALU.mult),
                 reads=[y, cosE], writes=[t1[k]])
            c.op("dve", lambda e: e.tensor_tensor(out=t2[k][:, 0:w], in0=pr[:, 0:w], in1=sinE[:, cl:cl + w], op=ALU.mult),
                 reads=[pr, sinE], writes=[t2[k]])
            c.op("dve", lambda e: e.tensor_tensor(out=o[:, lo:lo + w], in0=t1[k][:, 0:w], in1=t2[k][:, 0:w], op=ALU.add),
                 reads=[t1[k], t2[k]], writes=[o], join=True)
            if last:
                c.dma("sp", self.QKT[row * 128:(row + 1) * 128, c0:c0 + W], o[:, 0:W], reads=[o])

        def stage2():
            if kind == "C":
                c.op("pe", lambda e: e.matmul(pn[:, 0:w], self.ones_bf[:, :], sqb[k][:, 0:w], start=True, stop=True),
                     reads=[sqb[k], self.ones_bf], writes=[pn])
                c.op("act", lambda e: e.activation(out=rs[k][:, 0:w], in_=pn[:, 0:w], func=AF.Sqrt, scale=1.0 / 128, bias=self.eps_t[:, 0:1]),
                     reads=[pn, self.eps_t], writes=[rs[k]])
                c.op("dve", lambda e: e.reciprocal(out=rs[k][:, 0:w], in_=rs[k][:, 0:w]), reads=[rs[k]], writes=[rs[k]])
                c.op("dve", lambda e: e.scalar_tensor_tensor(out=xn[k][:, 0:w], in0=x[:, 0:w], scalar=gv[:, 0:1], in1=rs[k][:, 0:w],
                                                             op0=ALU.mult, op1=ALU.mult),
                     reads=[x, gv, rs[k]], writes=[xn[k]])
                y = xn[k]
            else:
                y = x
            return lambda: stage3(y)
        return stage2

    def phase_attn(self):
        c, I = self.c, self.I
        SC = 128 ** -0.5
        with ExitStack() as st:
            c.stack = st
            KT = [c.sb([128, SEQT], BF16, "KT") for _ in range(2)]
            Vt = [c.sb([128, 18, 128], BF16, "Vt") for _ in range(2)]
            QT = [c.sb([128, LAT], BF16, "QT") for _ in range(2)]
            PT = [c.sb([128, 640], BF16, "PT") for _ in range(3)]
            Pf = [c.sb([128, 384], F32, "Pf") for _ in range(2)]
            rden = [c.sb([128, 512], F32, "rden") for _ in range(2)]
            oo = [c.sb([128, 512], BF16, "oo") for _ in range(2)]
            wm = c.sb([128, 3, 128], F32, "wm")
            snk = c.sb([128, 8], F32, "snk"); esk = c.sb([128, 8], F32, "esk")
            c.dma("sp", wm[:], I["wmask"][:, :, :], writes=[wm])
            c.dma("sp", snk[:], I["sink_row"][0:1, :].partition_broadcast(128), writes=[snk])
            c.op("act", lambda e: e.activation(out=esk[:], in_=snk[:], func=AF.Exp), reads=[snk], writes=[esk])
            pS = [c.ps() for _ in range(2)]
            pSb = [c.ps() for _ in range(2)]
            pO = [c.ps() for _ in range(2)]
            pD = [c.ps() for _ in range(2)]
            li = 0
            qi = 0
            ui = 0
            pti = 0
            for mode in ("C", "D"):
                krow0 = 8 if mode == "C" else 18
                qrow0 = 0 if mode == "C" else 10
                vcol0 = 0 if mode == "C" else 256
                orow0 = 0 if mode == "C" else 8
                for s in range(2):
                    for kh in range(2):
                        kt = KT[li % 2]; vt = Vt[li % 2]; li += 1
                        c.dma("sp", kt[:], self.QKT[(krow0 + kh) * 128:(krow0 + kh + 1) * 128, s * SEQT:(s + 1) * SEQT], writes=[kt])
                        c.dma("sp", vt[:], self.VT[s * SEQT:(s + 1) * SEQT, vcol0 + kh * 128: vcol0 + (kh + 1) * 128].rearrange("(n p) d -> p n d", p=128),
                              writes=[vt])
                        for qh in range(4):
                            h = kh * 4 + qh
                            qt = QT[qi % 2]; qi += 1
                            c.dma("sp", qt[:], self.QKT[(qrow0 + h) * 128:(qrow0 + h + 1) * 128, s * SEQT + CTXL:(s + 1) * SEQT], writes=[qt])
                            for qs in range(4):
                                po = pO[ui % 2]; pd = pD[ui % 2]; ui += 1
                                if mode == "C":
                                    def smm(kc):
                                        p = pS[kc % 2]
                                        c.op("pe", lambda e: e.matmul(p[:, 0:512], kt[:, kc * 128:(kc + 1) * 128], qt[:, qs * 512:(qs + 1) * 512],
                                                                      start=True, stop=True), reads=[kt, qt], writes=[p])
                                    smm(0)
                                    for kc in range(18):
                                        if kc + 1 < 18:
                                            smm(kc + 1)
                                        p = pS[kc % 2]
                                        pt = PT[pti % 3]; pti += 1
                                        c.op("act", lambda e: e.activation(out=pt[:, 0:512], in_=p[:, 0:512], func=AF.Exp, scale=SC), reads=[p], writes=[pt])
                                        c.op("pe", lambda e: e.matmul(po[:, 0:512], vt[:, kc, :], pt[:, 0:512], start=(kc == 0), stop=(kc == 17)),
                                             reads=[vt, pt], writes=[po], inc=False)
                                        c.op("pe", lambda e: e.matmul(pd[:, 0:512], self.ones_bf[:, :], pt[:, 0:512], start=(kc == 0), stop=(kc == 17)),
                                             reads=[self.ones_bf, pt], writes=[pd], inc=True)
                                    rd = rden[ui % 2]
                                    c.op("dve", lambda e: e.reciprocal(out=rd[:, :], in_=pd[:, 0:512]), reads=[pd], writes=[rd])
                                else:
                                    for qq in range(4):
                                        qb = qs * 4 + qq
                                        kcs = [k for k in (qb - 1, qb, qb + 1) if 0 <= k < 16]
                                        j0 = kcs[0] - (qb - 1)
                                        pa = pS[qb % 2]; pb = pSb[qb % 2]
                                        qsl = qt[:, qb * 128:(qb + 1) * 128]
                                        for k in kcs:
                                            j = k - (qb - 1)
                                            c.op("pe", lambda e: e.matmul(pa[:, j * 128:(j + 1) * 128], kt[:, (2 + k) * 128:(3 + k) * 128], qsl, start=True, stop=True),
                                                 reads=[kt, qt], writes=[pa], inc=False)
                                        for j in range(2):
                                            c.op("pe", lambda e: e.matmul(pb[:, j * 128:(j + 1) * 128], kt[:, j * 128:(j + 1) * 128], qsl, start=True, stop=True),
                                                 reads=[kt, qt], writes=[pb], inc=(j == 1))
                                        nl = len(kcs)
                                        pf = Pf[qb % 2]
                                        pt = PT[pti % 3]; pti += 1
                                        c.op("act", lambda e: e.activation(out=pf[:, 0:nl * 128], in_=pa[:, j0 * 128:(j0 + nl) * 128], func=AF.Exp, scale=SC),
                                             reads=[pa], writes=[pf])
                                        c.op("act", lambda e: e.activation(out=pt[:, 384:640], in_=pb[:, 0:256], func=AF.Exp, scale=SC),
                                             reads=[pb], writes=[pt], join=True)
                                        c.op("dve", lambda e: e.tensor_tensor(out=pt[:, 0:nl * 128].rearrange("p (a b) -> p a b", b=128),
                                                                              in0=pf[:, 0:nl * 128].rearrange("p (a b) -> p a b", b=128),
                                                                              in1=wm[:, j0:j0 + nl, :], op=ALU.mult),
                                             reads=[pf, wm], writes=[pt], join=True)
                                        ops = [(vt[:, 2 + k, :], pt[:, (k - kcs[0]) * 128:(k - kcs[0] + 1) * 128]) for k in kcs]
                                        ops += [(vt[:, j, :], pt[:, 384 + j * 128:384 + (j + 1) * 128]) for j in range(2)]
                                        n = len(ops)
                                        for i2, (va, pa2) in enumerate(ops):
                                            c.op("pe", lambda e: e.matmul(po[:, qq * 128:(qq + 1) * 128], va, pa2, start=(i2 == 0), stop=(i2 == n - 1)),
                                                 reads=[vt, pt], writes=[po], inc=False)
                                        for i2, (va, pa2) in enumerate(ops):
                                            c.op("pe", lambda e: e.matmul(pd[:, qq * 128:(qq + 1) * 128], self.ones_bf[:, :], pa2, start=(i2 == 0), stop=(i2 == n - 1)),
                                                 reads=[self.ones_bf, pt], writes=[pd], inc=(i2 == n - 1))
                                    rd = rden[ui % 2]
                                    c.op("dve", lambda e: e.tensor_scalar(out=rd[:, :], in0=pd[:, 0:512], scalar1=esk[:, h:h + 1], scalar2=None, op0=ALU.add),
                                         reads=[pd, esk], writes=[rd])
                                    c.op("dve", lambda e: e.reciprocal(out=rd[:, :], in_=rd[:, :]), reads=[rd], writes=[rd])
                                o = oo[ui % 2]
                                c.op("dve", lambda e: e.tensor_tensor(out=o[:, :], in0=po[:, 0:512], in1=rd[:, :], op=ALU.mult), reads=[po, rd], writes=[o])
                                c0 = s * SEQT + CTXL + qs * 512
                                c.dma("sp", self.YAB[(orow0 + h) * 128:(orow0 + h + 1) * 128, c0:c0 + 512], o[:, :], reads=[o])

    def build(self):
        c = self.c
        self.declare()
        with ExitStack() as gst:
            c.stack = gst
            self.persistent()
            self._run()
            if not self.done:
                c.barrier()
        return self.nc

    def _run(self):
        I = self.I
        self.phase_mod()
        if "MODT" in self.dbg:
            t = self.nc.dram_tensor("MODT", [128, 2 * 96 * 3], F32, kind="ExternalOutput")
            self.c.dma("sp", t[:, :], self.modT[:].rearrange("p a b c -> p (a b c)"), reads=[self.modT])
            t2 = self.nc.dram_tensor("GSC", [128, 2 * 2 * 16 * 3], F32, kind="ExternalOutput")
            self.c.dma("sp", t2[:, :], self.gsc[:].rearrange("p a b c d -> p (a b c d)"), reads=[self.gsc])
            t3 = self.nc.dram_tensor("GG", [128, 2 * 2 * 16 * 3], F32, kind="ExternalOutput")
            self.c.dma("sp", t3[:, :], self.gg[:].rearrange("p a b c d -> p (a b c d)"), reads=[self.gg])
        if self.end_phase("mod"):
            return
        h0 = I["hT"]
        if 0 in self.layers:
            from_l0 = self.layer0(h0)
            if self.done:
                return
            h_in = self.HB
        else:
            h_in = h0
        if 1 in self.layers:
            self.phase_cd_inproj(h_in)
            if self.end_phase("cd_in"):
                return
            self.phase_attn()
            if self.end_phase("attn"):
                return
            self.phase_outproj(1, I["cd_w_out"], 16, h_in, self.HA, tiles_lat())
            if self.end_phase("cd_out"):
                return
            self.phase_mlp(1, self.HA, self.OUT, tiles_lat(), lambda c0: c0 - CTXL * (c0 // SEQT + 1))
            if self.end_phase("mlp1"):
                return

    def phase_ab_inproj(self, hsrc):
        c, I = self.c, self.I
        wi = I["ab_w_in"]
        AXX = mybir.AxisListType.X
        with ExitStack() as st:
            c.stack = st
            W = 768
            S = self.norm_scratch(W)
            aTs = [c.sb([128, 16, W], BF16, "aT") for _ in range(2)]
            cur = [aTs[0]]
            Wv = c.sb([128, 16, 2048], BF16, "Wv")
            Wdt = c.sb([128, 16, 64], BF16, "Wdt")
            wbs = [c.sb([128, 16, 256], BF16, "wi") for _ in range(2)]
            lng = c.sb([128, 2048], F32, "lng"); lnb = c.sb([128, 2048], F32, "lnb")
            c.dma("sp", lng[:], I["lng_row"][0:1, :].partition_broadcast(128), writes=[lng])
            c.dma("sp", lnb[:], I["lnb_row"][0:1, :].partition_broadcast(128), writes=[lnb])
            for q in range(4):
                self.load_w_cols(Wv, q * 512, wi[:, 2048 + q * 512: 2048 + (q + 1) * 512], 16, 512, first=(q == 0))
            self.load_w(Wdt, wi[:, 9216:9280], 16, 64)
            ob = [c.sb([128, W], BF16, "ob") for _ in range(2)]
            obf = [c.sb([128, W], F32, "obf") for _ in range(2)]
            vg = c.sb([128, 2048], F32, "vg"); vn = c.sb([128, 2048], F32, "vn"); vbf = c.sb([128, 2048], BF16, "vbf")
            st4 = c.sb([128, 8], F32, "st4")
            dto = c.sb([128, 6, 64], F32, "dto")
            pg = [c.ps() for _ in range(2)]
            pv = [c.ps() for _ in range(4)]
            ftiles = [(q * 256, "u", self.U, q * 256) for q in range(8)]
            ftiles += [(4096 + q * 256, "z", self.SZ, q * 256) for q in range(8)]
            ftiles += [(6144 + q * 256, "x", self.XBC, q * 256) for q in range(12)]
            st_ei = [0, 0]
            def vblock(tb, c0):
                aT = cur[0]
                asl = lambda kc: aT[:, kc, tb * 128:(tb + 1) * 128]
                for cb in range(4):
                    for kc in range(16):
                        c.op("pe", lambda e: e.matmul(pv[cb][:, 0:512], asl(kc), Wv[:, kc, cb * 512:(cb + 1) * 512],
                                                      start=(kc == 0), stop=(kc == 15)),
                             reads=[Wv, aT], writes=[pv[cb]], inc=(kc == 15))
                    c.op("act", lambda e: e.activation(out=vg[:, cb * 512:(cb + 1) * 512], in_=pv[cb][:, 0:512], func=AF.Gelu),
                         reads=[pv[cb]], writes=[vg], join=(cb > 0))
                p = pg[st_ei[0] % 2]; st_ei[0] += 1
                for kc in range(16):
                    c.op("pe", lambda e: e.matmul(p[:, 0:64], asl(kc), Wdt[:, kc, :], start=(kc == 0), stop=(kc == 15)),
                         reads=[Wdt, aT], writes=[p], inc=(kc == 15))
                c.op("act", lambda e: e.activation(out=dto[:, tb, :], in_=p[:, 0:64], func=AF.Copy), reads=[p], writes=[dto], join=(tb > 0))
                c.op("dve", lambda e: e.reduce_sum(out=st4[:, 0:1], in_=vg[:, :], axis=AXX), reads=[vg], writes=[st4])
                c.op("act", lambda e: e.activation(out=vn[:, :], in_=vg[:, :], func=AF.Square), reads=[vg], writes=[vn])
                c.op("dve", lambda e: e.reduce_sum(out=st4[:, 1:2], in_=vn[:, :], axis=AXX), reads=[vn], writes=[st4])
                c.op("dve", lambda e: e.tensor_scalar(out=st4[:, 2:3], in0=st4[:, 0:1], scalar1=1.0 / 2048, scalar2=None, op0=ALU.mult),
                     reads=[st4], writes=[st4])
                c.op("dve", lambda e: e.tensor_tensor(out=st4[:, 3:4], in0=st4[:, 2:3], in1=st4[:, 2:3], op=ALU.mult), reads=[st4], writes=[st4])
                c.op("dve", lambda e: e.scalar_tensor_tensor(out=st4[:, 4:5], in0=st4[:, 1:2], scalar=1.0 / 2048, in1=st4[:, 3:4],
                                                             op0=ALU.mult, op1=ALU.subtract), reads=[st4], writes=[st4])
                c.op("act", lambda e: e.activation(out=st4[:, 5:6], in_=st4[:, 4:5], func=AF.Sqrt, bias=self.eps_t[:, 0:1]),
                     reads=[st4, self.eps_t], writes=[st4])
                c.op("dve", lambda e: e.reciprocal(out=st4[:, 6:7], in_=st4[:, 5:6]), reads=[st4], writes=[st4])
                c.op("dve", lambda e: e.tensor_scalar(out=vn[:, :], in0=vg[:, :], scalar1=st4[:, 2:3], scalar2=st4[:, 6:7],
                                                      op0=ALU.subtract, op1=ALU.mult), reads=[vg, st4], writes=[vn])
                c.op("pool", lambda e: e.tensor_tensor(out=vn[:, :], in0=vn[:, :], in1=lng[:, :], op=ALU.mult), reads=[vn, lng], writes=[vn])
                c.op("pool", lambda e: e.tensor_tensor(out=vbf[:, :], in0=vn[:, :], in1=lnb[:, :], op=ALU.add), reads=[vn, lnb], writes=[vbf])
                c.dma("sp", self.V[c0 + tb * 128:c0 + (tb + 1) * 128, :], vbf[:, :], reads=[vbf])

            tl = tiles_all()
            self.load_modnorm(S, hsrc, tl[0][0], W, 0, 0, aTs[0])
            for ti, (c0, _) in enumerate(tl):
                aT = aTs[ti % 2]; cur[0] = aT
                pend_norm = None
                if ti + 1 < len(tl):
                    pend_norm = self.gen_modnorm(S, hsrc, tl[ti + 1][0], W, 0, 0, aTs[(ti + 1) % 2])
                sb = subs(W)
                self.load_w(wbs[0], wi[:, ftiles[0][0]:ftiles[0][0] + 256], 16, 256, cache=self.ABIB[:, ftiles[0][0]:ftiles[0][0] + 256], key=("abi", 0))
                for t, (col0, kind, dst, row0) in enumerate(ftiles):
                    if t + 1 < len(ftiles):
                        nc0 = ftiles[t + 1][0]
                        self.load_w(wbs[(t + 1) % 2], wi[:, nc0:nc0 + 256], 16, 256, cache=self.ABIB[:, nc0:nc0 + 256], key=("abi", t + 1))
                    wb = wbs[t % 2]
                    for oi in range(2):
                        o = (obf if kind == "x" else ob)[st_ei[1] % 2]; st_ei[1] += 1
                        for (lo, w) in sb:
                            if pend_norm is not None and t >= 4:
                                next(pend_norm, None)
                            p = pg[st_ei[0] % 2]; st_ei[0] += 1
                            for kc in range(16):
                                c.op("pe", lambda e: e.matmul(p[:, 0:w], wb[:, kc, oi * 128:(oi + 1) * 128], aT[:, kc, lo:lo + w],
                                                              start=(kc == 0), stop=(kc == 15)),
                                     reads=[wb, aT], writes=[p], inc=(kc == 15))
                            fn = {"u": AF.Gelu, "z": AF.Silu, "x": AF.Copy}[kind]
                            c.op("act", lambda e: e.activation(out=o[:, lo:lo + w], in_=p[:, 0:w], func=fn), reads=[p], writes=[o], join=True)
                        r0 = row0 + oi * 128
                        c.dma("sp", dst[r0:r0 + 128, c0:c0 + W], o[:, 0:W], reads=[o])
                    if t % 4 == 3 and t // 4 < 6:
                        vblock(t // 4, c0)
                if pend_norm is not None:
                    for _ in pend_norm:
                        pass
                c.dma("sp", self.DTR[c0:c0 + W, :].rearrange("(n p) d -> p n d", p=128), dto[:, :, :], reads=[dto])

    def load_w_cols(self, wb, col0, src2d, KC, ncols, first=True):
        c = self.c
        for k0 in range(0, KC, 4):
            c.dma("pool", wb[:, k0:k0 + 4, col0:col0 + ncols],
                  src2d[k0 * 128:(k0 + 4) * 128, :].rearrange("(kc p) n -> p kc n", p=128),
                  writes=[wb], join=not (first and k0 == 0))

    def phase_gmlp(self):
        c, I = self.c, self.I
        with ExitStack() as st:
            c.stack = st
            wsT = c.sb([128, 8, 128], BF16, "wsT")
            c.dma("pool", wsT[:], I["wsT"][:, :, :], writes=[wsT])
            Bf = c.sb([128, 16, 128], F32, "Bf")
            c.dma("sp", Bf[:].rearrange("p a b -> p (a b)"), I["bs_row"][0:1, :].partition_broadcast(128), writes=[Bf])
            Ut = [c.sb([128, 16, 512], BF16, "Ut") for _ in range(2)]
            Vc = [c.sb([128, 2048], BF16, "Vc") for _ in range(2)]
            yo = [c.sb([128, 16, 512], BF16, "yo") for _ in range(2)]
            tf = [c.sb([128, 4, 128], F32, "tf") for _ in range(2)]
            pp = [c.ps() for _ in range(8)]
            vi = 0
            ti = 0
            for gi in range(NT // 512):
                g0 = gi * 512
                ut = Ut[gi % 2]; y = yo[gi % 2]
                for k0 in range(0, 16, 4):
                    c.dma("sp", ut[:, k0:k0 + 4, :], self.U[k0 * 128:(k0 + 4) * 128, g0:g0 + 512].rearrange("(kc p) n -> p kc n", p=128),
                          writes=[ut], join=(k0 > 0))
                for j in range(4):
                    vc = Vc[vi % 2]
                    c.dma("sp", vc[:, :], self.V[g0 + j * 128:g0 + (j + 1) * 128, :], writes=[vc])
                    for blk in range(16):
                        p = pp[(vi % 2) * 4 + blk // 4]
                        c.op("pe", lambda e: e.matmul(p[:, (blk % 4) * 128:(blk % 4 + 1) * 128], vc[:, blk * 128:(blk + 1) * 128], wsT[:, blk // 2, :],
                                                      start=True, stop=True),
                             reads=[vc, wsT], writes=[p], inc=(blk % 4 == 3))
                    for b in range(4):
                        p = pp[(vi % 2) * 4 + b]
                        t = tf[ti % 2]; ti += 1
                        c.op("dve", lambda e: e.tensor_tensor(out=t[:, :, :], in0=p[:, 0:512].rearrange("p (a b) -> p a b", b=128),
                                                              in1=Bf[:, 4 * b:4 * b + 4, :], op=ALU.add), reads=[p, Bf], writes=[t])
                        c.op("pool", lambda e: e.tensor_tensor(out=y[:, 4 * b:4 * b + 4, j * 128:(j + 1) * 128], in0=t[:, :, :],
                                                               in1=ut[:, 4 * b:4 * b + 4, j * 128:(j + 1) * 128], op=ALU.mult),
                             reads=[t, ut], writes=[y], join=True)
                    vi += 1
                for k0 in range(0, 16, 4):
                    c.dma("sp", self.YAB[k0 * 128:(k0 + 4) * 128, g0:g0 + 512].rearrange("(kc p) n -> p kc n", p=128), y[:, k0:k0 + 4, :], reads=[y])

    def phase_conv(self):
        c, I = self.c, self.I
        with ExitStack() as st:
            c.stack = st
            cw = c.sb([128, 24, 5], F32, "cw"); cb = c.sb([128, 24], F32, "cb")
            c.dma("sp", cw[:], I["conv_w"][:, :, :], writes=[cw]); c.dma("sp", cb[:], I["conv_b"][:, :], writes=[cb])
            xb = [c.sb([128, SEQT], F32, "xb") for _ in range(3)]
            acc = [c.sb([128, SEQT], F32, "acc") for _ in range(3)]
            it = 0
            for s in range(2):
                for cc in range(24):
                    x = xb[it % 3]; a = acc[it % 3]; it += 1
                    c.dma("sp", x[:, :], self.XBC[cc * 128:(cc + 1) * 128, s * SEQT:(s + 1) * SEQT], writes=[x])
                    c.op("act", lambda e: e.activation(out=a[:, :], in_=x[:, :], func=AF.Identity, scale=cw[:, cc, 2:3], bias=cb[:, cc:cc + 1]),
                         reads=[x, cw, cb], writes=[a])
                    for k in (0, 1, 3, 4):
                        dlt = k - 2
                        for (sa, sbb) in ((0, CTXL), (CTXL, SEQT)):
                            lo = max(sa, sa - dlt); hi = min(sbb, sbb - dlt)
                            c.op("dve", lambda e: e.scalar_tensor_tensor(out=a[:, lo:hi], in0=x[:, lo + dlt:hi + dlt], scalar=cw[:, cc, k:k + 1],
                                                                         in1=a[:, lo:hi], op0=ALU.mult, op1=ALU.add),
                                 reads=[x, cw, a], writes=[a])
                    c.op("act", lambda e: e.activation(out=x[:, :], in_=a[:, :], func=AF.Silu), reads=[a], writes=[x])
                    c.dma("sp", self.XC[cc * 128:(cc + 1) * 128, s * SEQT:(s + 1) * SEQT], x[:, :], reads=[x])

    def phase_ssd(self):
        c, I = self.c, self.I
        with ExitStack() as st:
            c.stack = st
            tri = [c.sb([128, 128], F32, "tri") for _ in range(2)]
            ident = c.sb([128, 128], F32, "ident")
            c.dma("sp", tri[0][:], I["tri_f"][:, :], writes=[tri[0]]); c.dma("sp", tri[1][:], I["tri_b"][:, :], writes=[tri[1]])
            c.dma("sp", ident[:], I["ident"][:, :], writes=[ident])
            a_b = c.sb([128, 64], F32, "a_b"); dtb = c.sb([128, 64], F32, "dtb")
            c.dma("sp", a_b[:], I["alog_row"][0:1, :].partition_broadcast(128), writes=[a_b])
            c.dma("sp", dtb[:], I["dtb_row"][0:1, :].partition_broadcast(128), writes=[dtb])
            c.op("act", lambda e: e.activation(out=a_b[:], in_=a_b[:], func=AF.Exp), reads=[a_b], writes=[a_b])
            c.op("dve", lambda e: e.tensor_scalar(out=a_b[:], in0=a_b[:], scalar1=-1.0, scalar2=None, op0=ALU.mult), reads=[a_b], writes=[a_b])
            bd = c.sb([128, 16], F32, "bd"); gng = c.sb([128, 16], F32, "gng")
            c.dma("sp", bd[:], I["bd"][:, :], writes=[bd]); c.dma("sp", gng[:], I["gng"][:, :], writes=[gng])
            dt = c.sb([128, 18, 64], F32, "dt"); dta = c.sb([128, 18, 64], F32, "dta")
            hT = c.sb([128, 2048], F32, "hT"); hTb = c.sb([128, 2048], BF16, "hTb")

            def slot():
                d = {}
                d["xs"] = c.sb([128, 16, 128], F32, "xsT")
                d["b"] = c.sb([128, 4, 128], F32, "bT"); d["c"] = c.sb([128, 4, 128], F32, "cT")
                d["bb"] = c.sb([128, 4, 128], BF16, "bTb"); d["cb"] = c.sb([128, 4, 128], BF16, "cTb")
                d["yf"] = c.sb([128, 16, 128], F32, "yfl"); d["sz"] = c.sb([128, 16, 128], BF16, "szl")
                d["cum"] = c.sb([128, 32], F32, "cum"); d["te"] = c.sb([128, 32], F32, "te"); d["cd"] = c.sb([128, 32], F32, "cd")
                d["tm"] = c.sb([128, 32], F32, "tm32")
                d["xdt"] = c.sb([128, 2048], F32, "xdt"); d["xdtb"] = c.sb([128, 2048], BF16, "xdtb"); d["xteb"] = c.sb([128, 2048], BF16, "xteb")
                d["btok"] = c.sb([128, 512], BF16, "btok"); d["cbm"] = c.sb([128, 4, 128], F32, "cbm")
                d["yo"] = c.sb([128, 16, 128], F32, "yo")
                return d
            SL = [slot(), slot()]
            sg = [c.sb([128, 4, 128], F32, "sg") for _ in range(2)]
            Ee = [c.sb([128, 4, 128], F32, "Ee") for _ in range(2)]
            ecb = [c.sb([128, 4, 128], F32, "ecb") for _ in range(2)]
            MT = [c.sb([128, 4, 128], BF16, "MT") for _ in range(2)]
            crT = [c.sb([128, 4, 128], BF16, "crT") for _ in range(2)]
            y2 = c.sb([128, 16, 128], F32, "y2"); sqy = c.sb([128, 16, 128], BF16, "sqy")
            rsn = c.sb([128, 4, 128], F32, "rsn"); ybf = c.sb([128, 16, 128], BF16, "ybf")
            pY = [c.ps() for _ in range(4)]
            pE = [c.ps() for _ in range(2)]
            pM = [c.ps() for _ in range(2)]
            st_ = {"mi": 0}

            def nextM():
                p = pM[st_["mi"] % 2]; st_["mi"] += 1
                return p

            def prep(s, di, n, d):
                T = tri[di]; h0 = di * 32
                col = s * SEQT + n * 128
                xs, b_, c_ = d["xs"], d["b"], d["c"]
                for k0 in range(0, 16, 4):
                    c.dma("sp", xs[:, k0:k0 + 4, :], self.XC[k0 * 128:(k0 + 4) * 128, col:col + 128].rearrange("(kc p) n -> p kc n", p=128),
                          writes=[xs], join=(k0 > 0))
                c.dma("sp", b_[:, :, :], self.XC[2048:2560, col:col + 128].rearrange("(kc p) n -> p kc n", p=128), writes=[b_])
                c.dma("sp", c_[:, :, :], self.XC[2560:3072, col:col + 128].rearrange("(kc p) n -> p kc n", p=128), writes=[c_])
                if di == 1:
                    for k0 in range(0, 16, 4):
                        c.dma("sp", d["yf"][:, k0:k0 + 4, :], self.YF[k0 * 128:(k0 + 4) * 128, col:col + 128].rearrange("(kc p) n -> p kc n", p=128),
                              writes=[d["yf"]], join=(k0 > 0))
                        c.dma("sp", d["sz"][:, k0:k0 + 4, :], self.SZ[k0 * 128:(k0 + 4) * 128, col:col + 128].rearrange("(kc p) n -> p kc n", p=128),
                              writes=[d["sz"]], join=(k0 > 0))
                c.op("act", lambda e: e.activation(out=d["bb"][:, :, :], in_=b_[:, :, :], func=AF.Copy), reads=[b_], writes=[d["bb"]])
                c.op("act", lambda e: e.activation(out=d["cb"][:, :, :], in_=c_[:, :, :], func=AF.Copy), reads=[c_], writes=[d["cb"]])
                yield
                dsl = dta[:, n, h0:h0 + 32]
                pa = nextM()
                c.op("pe", lambda e: e.matmul(pa[:, 0:32], T[:, :], dsl, start=True, stop=True), reads=[T, dta], writes=[pa], inc=False)
                c.op("pe", lambda e: e.matmul(pa[:, 32:64], self.ones_f[:, :], dsl, start=True, stop=True), reads=[self.ones_f, dta], writes=[pa])
                c.op("act", lambda e: e.activation(out=d["cum"][:, :], in_=pa[:, 0:32], func=AF.Copy), reads=[pa], writes=[d["cum"]])
                c.op("dve", lambda e: e.tensor_tensor(out=d["tm"][:, :], in0=pa[:, 32:64], in1=d["cum"][:, :], op=ALU.subtract), reads=[pa, d["cum"]], writes=[d["tm"]])
                c.op("act", lambda e: e.activation(out=d["te"][:, :], in_=d["tm"][:, :], func=AF.Exp), reads=[d["tm"]], writes=[d["te"]])
                c.op("act", lambda e: e.activation(out=d["cd"][:, :], in_=pa[:, 32:64], func=AF.Exp), reads=[pa], writes=[d["cd"]])
                for b in range(4):
                    yield
                    pt = nextM()
                    for q in range(4):
                        cc = b * 4 + q
                        c.op("pe", lambda e: e.transpose(pt[:, q * 128:(q + 1) * 128], xs[:, cc, :], ident[:, :]), reads=[xs, ident], writes=[pt], inc=(q == 3))
                    c.op("dve", lambda e: e.tensor_tensor(out=d["xdt"][:, b * 512:(b + 1) * 512].rearrange("p (a b) -> p a b", b=64),
                                                          in0=pt[:, 0:512].rearrange("p (a b) -> p a b", b=64),
                                                          in1=dt[:, n, h0 + b * 8:h0 + b * 8 + 8].unsqueeze(2).to_broadcast([128, 8, 64]), op=ALU.mult),
                         reads=[pt, dt], writes=[d["xdt"]], join=(b > 0))
                yield
                c.op("act", lambda e: e.activation(out=d["xdtb"][:, :], in_=d["xdt"][:, :], func=AF.Copy), reads=[d["xdt"]], writes=[d["xdtb"]])
                c.op("pool", lambda e: e.tensor_tensor(out=d["xteb"][:, :].rearrange("p (a b) -> p a b", b=64),
                                                       in0=d["xdt"][:, :].rearrange("p (a b) -> p a b", b=64),
                                                       in1=d["te"][:, :].unsqueeze(2).to_broadcast([128, 32, 64]), op=ALU.mult),
                     reads=[d["xdt"], d["te"]], writes=[d["xteb"]])
                yield
                pb = nextM()
                for g in range(4):
                    c.op("pe", lambda e: e.transpose(pb[:, g * 128:(g + 1) * 128], b_[:, g, :], ident[:, :]), reads=[b_, ident], writes=[pb], inc=(g == 3))
                c.op("act", lambda e: e.activation(out=d["btok"][:, :], in_=pb[:, 0:512], func=AF.Copy), reads=[pb], writes=[d["btok"]])
                yield
                pc = nextM()
                for g in range(4):
                    c.op("pe", lambda e: e.matmul(pc[:, g * 128:(g + 1) * 128], d["bb"][:, g, :], d["cb"][:, g, :], start=True, stop=True),
                         reads=[d["bb"], d["cb"]], writes=[pc], inc=(g == 3))
                c.op("dve", lambda e: e.tensor_tensor(out=d["cbm"][:, :, :], in0=pc[:, 0:512].rearrange("p (a b) -> p a b", b=128),
                                                      in1=T[:, :].unsqueeze(1).to_broadcast([128, 4, 128]), op=ALU.mult),
                     reads=[pc, T], writes=[d["cbm"]])
                yield
                if di == 1:
                    c.op("pool", lambda e: e.tensor_tensor(out=d["yo"][:, :, :], in0=xs[:, :, :], in1=bd[:, :].unsqueeze(2).to_broadcast([128, 16, 128]), op=ALU.mult),
                         reads=[xs, bd], writes=[d["yo"]])
                    c.op("pool", lambda e: e.tensor_tensor(out=d["yo"][:, :, :], in0=d["yo"][:, :, :], in1=d["yf"][:, :, :], op=ALU.add),
                         reads=[d["yo"], d["yf"]], writes=[d["yo"]])

            def main(s, di, n, d, pend):
                T = tri[di]; h0 = di * 32
                col = s * SEQT + n * 128
                c_ = d["c"]

                def cumb(t):
                    pe_ = pE[t % 2]
                    for r4 in range(4):
                        hh = t * 4 + r4
                        c.op("pe", lambda e: e.matmul(pe_[:, r4 * 128:(r4 + 1) * 128], dta[:, n, h0 + hh:h0 + hh + 1].to_broadcast([128, 128]), T[:, :],
                                                      start=True, stop=True), reads=[dta, T], writes=[pe_], inc=(r4 == 3))
                cumb(0)
                for t in range(8):
                    if t + 1 < 8:
                        cumb(t + 1)
                    k = t % 2; g = t // 2
                    pe_ = pE[k]
                    for r4 in range(4):
                        hh = t * 4 + r4
                        c.op("dve", lambda e: e.scalar_tensor_tensor(out=sg[k][:, r4, :], in0=pe_[:, r4 * 128:(r4 + 1) * 128], scalar=d["cum"][:, hh:hh + 1],
                                                                     in1=T[:, :], op0=ALU.subtract, op1=ALU.mult),
                             reads=[pe_, d["cum"], T], writes=[sg[k]], join=(r4 > 0))
                    c.op("act", lambda e: e.activation(out=Ee[k][:, :, :], in_=sg[k][:, :, :], func=AF.Exp), reads=[sg[k]], writes=[Ee[k]])
                    c.op("act", lambda e: e.activation(out=ecb[k][:, :, :], in_=pe_[:, 0:512].rearrange("p (a b) -> p a b", b=128), func=AF.Exp),
                         reads=[pe_], writes=[ecb[k]])
                    c.op("dve", lambda e: e.tensor_tensor(out=MT[k][:, :, :], in0=Ee[k][:, :, :], in1=d["cbm"][:, g:g + 1, :].to_broadcast([128, 4, 128]), op=ALU.mult),
                         reads=[Ee[k], d["cbm"]], writes=[MT[k]])
                    c.op("pool", lambda e: e.tensor_tensor(out=crT[k][:, :, :], in0=ecb[k][:, :, :], in1=c_[:, g:g + 1, :].to_broadcast([128, 4, 128]), op=ALU.mult),
                         reads=[ecb[k], c_], writes=[crT[k]])
                    for r4 in range(4):
                        hh = t * 4 + r4
                        cc = hh // 2; half = hh % 2
                        out = pY[cc // 4][64 * half:64 * half + 64, (cc % 4) * 128:(cc % 4 + 1) * 128]
                        c.op("pe", lambda e: e.matmul(out, d["xdtb"][:, hh * 64:(hh + 1) * 64], MT[k][:, r4, :], start=True, stop=False),
                             reads=[d["xdtb"], MT[k]], writes=[pY[cc // 4]], inc=False)
                        c.op("pe", lambda e: e.matmul(out, hTb[:, hh * 64:(hh + 1) * 64], crT[k][:, r4, :], start=False, stop=True),
                             reads=[hTb, crT[k]], writes=[pY[cc // 4]], inc=True)
                if pend is not None:
                    for _ in pend:
                        pass
                for g in range(4):
                    psn = nextM()
                    c.op("pe", lambda e: e.matmul(psn[:, 0:512], d["btok"][:, g * 128:(g + 1) * 128], d["xteb"][:, g * 512:(g + 1) * 512], start=True, stop=True),
                         reads=[d["btok"], d["xteb"]], writes=[psn])
                    hv = hT[:, g * 512:(g + 1) * 512]
                    c.op("dve", lambda e: e.tensor_tensor(out=hv.rearrange("p (a b) -> p a b", b=64), in0=hv.rearrange("p (a b) -> p a b", b=64),
                                                          in1=d["cd"][:, g * 8:(g + 1) * 8].unsqueeze(2).to_broadcast([128, 8, 64]), op=ALU.mult),
                         reads=[hT, d["cd"]], writes=[hT])
                    c.op("dve", lambda e: e.tensor_tensor(out=hv, in0=psn[:, 0:512], in1=hv, op=ALU.add), reads=[psn, hT], writes=[hT])
                c.op("act", lambda e: e.activation(out=hTb[:, :], in_=hT[:, :], func=AF.Copy), reads=[hT], writes=[hTb])
                if di == 0:
                    yout = d["yo"]
                    for b in range(4):
                        c.op("act", lambda e: e.activation(out=yout[:, 4 * b:4 * b + 4, :], in_=pY[b][:, 0:512].rearrange("p (a b) -> p a b", b=128), func=AF.Copy),
                             reads=[pY[b]], writes=[yout], join=(b > 0))
                    for k0 in range(0, 16, 4):
                        c.dma("sp", self.YF[k0 * 128:(k0 + 4) * 128, col:col + 128].rearrange("(kc p) n -> p kc n", p=128), yout[:, k0:k0 + 4, :], reads=[yout])
                else:
                    for b in range(4):
                        c.op("dve", lambda e: e.tensor_tensor(out=y2[:, 4 * b:4 * b + 4, :], in0=pY[b][:, 0:512].rearrange("p (a b) -> p a b", b=128),
                                                              in1=d["yo"][:, 4 * b:4 * b + 4, :], op=ALU.add), reads=[pY[b], d["yo"]], writes=[y2], join=(b > 0))
                    c.op("dve", lambda e: e.tensor_tensor(out=y2[:, :, :], in0=y2[:, :, :], in1=d["sz"][:, :, :], op=ALU.mult), reads=[y2, d["sz"]], writes=[y2])
                    c.op("act", lambda e: e.activation(out=sqy[:, :, :], in_=y2[:, :, :], func=AF.Square), reads=[y2], writes=[sqy])
                    pn = nextM()
                    for g in range(4):
                        for j in range(4):
                            c.op("pe", lambda e: e.matmul(pn[:, g * 128:(g + 1) * 128], self.ones_bf[:, :], sqy[:, g * 4 + j, :], start=(j == 0), stop=(j == 3)),
                                 reads=[self.ones_bf, sqy], writes=[pn], inc=(g == 3 and j == 3))
                    c.op("act", lambda e: e.activation(out=rsn[:, :, :], in_=pn[:, 0:512].rearrange("p (a b) -> p a b", b=128), func=AF.Sqrt,
                                                       scale=1.0 / 512, bias=self.eps_t[:, 0:1]), reads=[pn, self.eps_t], writes=[rsn])
                    c.op("dve", lambda e: e.reciprocal(out=rsn[:, :, :], in_=rsn[:, :, :]), reads=[rsn], writes=[rsn])
                    for cc in range(16):
                        c.op("dve", lambda e: e.scalar_tensor_tensor(out=ybf[:, cc, :], in0=y2[:, cc, :], scalar=gng[:, cc:cc + 1], in1=rsn[:, cc // 4, :],
                                                                     op0=ALU.mult, op1=ALU.mult),
                             reads=[y2, gng, rsn], writes=[ybf], join=(cc > 0))
                    for k0 in range(0, 16, 4):
                        c.dma("sp", self.YAB[2048 + k0 * 128:2048 + (k0 + 4) * 128, col:col + 128].rearrange("(kc p) n -> p kc n", p=128), ybf[:, k0:k0 + 4, :], reads=[ybf])

            gi = 0
            for s in range(2):
                c.dma("sp", dt[:, :, :], self.DTR[s * SEQT:(s + 1) * SEQT, :].rearrange("(n p) d -> p n d", p=128), writes=[dt])
                c.op("dve", lambda e: e.tensor_tensor(out=dt[:, :, :], in0=dt[:, :, :], in1=dtb[:, :].unsqueeze(1).to_broadcast([128, 18, 64]), op=ALU.add),
                     reads=[dt, dtb], writes=[dt])
                c.op("act", lambda e: e.activation(out=dt[:, :, :], in_=dt[:, :, :], func=AF.Exp), reads=[dt], writes=[dt])
                c.op("act", lambda e: e.activation(out=dt[:, :, :], in_=dt[:, :, :], func=AF.Ln, bias=self.ones_f[:, 0:1]), reads=[dt, self.ones_f], writes=[dt])
                c.op("dve", lambda e: e.tensor_tensor(out=dta[:, :, :], in0=dt[:, :, :], in1=a_b[:, :].unsqueeze(1).to_broadcast([128, 18, 64]), op=ALU.mult),
                     reads=[dt, a_b], writes=[dta])
                for di in range(2):
                    order = [0, 1] + list(range(2, 18)) if di == 0 else [1, 0] + list(range(17, 1, -1))
                    c.op("dve", lambda e: e.memset(hT[:, :], 0.0), writes=[hT])
                    c.op("dve", lambda e: e.memset(hTb[:, :], 0.0), writes=[hTb])
                    for _ in prep(s, di, order[0], SL[gi % 2]):
                        pass
                    for idx, n in enumerate(order):
                        pend = prep(s, di, order[idx + 1], SL[(gi + 1) % 2]) if idx + 1 < len(order) else None
                        main(s, di, n, SL[gi % 2], pend)
                        gi += 1

    def layer0(self, h0):
        I = self.I
        self.phase_ab_inproj(h0)
        if self.end_phase("ab_in"):
            return
        self.phase_gmlp()
        if self.end_phase("gmlp"):
            return
        self.phase_conv()
        if self.end_phase("conv"):
            return
        self.phase_ssd()
        if self.end_phase("ssd"):
            return
        self.phase_outproj(0, I["ab_w_out"], 32, h0, self.HA, tiles_all())
        if self.end_phase("ab_out"):
            return
        self.phase_mlp(0, self.HA, self.HB, tiles_all(), lambda c0: c0)
        if self.end_phase("mlp0"):
            return


def _consts():
    ident = np.eye(128, dtype=np.float32)
    k = np.arange(128)
    tri_f = (k[:, None] <= k[None, :]).astype(np.float32)
    tri_b = (k[:, None] >= k[None, :]).astype(np.float32)
    R = np.zeros((128, 128), np.float32)
    for d in range(128):
        half = (d % 64) // 32
        if half == 0:
            R[d, d + 32] = -1.0
        else:
            R[d, d - 32] = 1.0
    rotT = np.ascontiguousarray(R.T)
    t = np.arange(LAT)
    pos = np.stack([t // 64, t % 64], axis=-1).astype(np.float32)
    inv_freq = (10000.0 ** (-np.arange(0, 64, 2, dtype=np.float32) / 64)).astype(np.float32)
    ang = pos[:, :, None] * inv_freq
    cosE = np.ones((128, SEQT), np.float32); sinE = np.zeros((128, SEQT), np.float32)
    for d in range(128):
        ax = d // 64; f = d % 32
        cosE[d, CTXL:] = np.cos(ang[:, ax, f]); sinE[d, CTXL:] = np.sin(ang[:, ax, f])
    wmask = np.ones((128, 3, 128), np.float32)
    wmask[:, 0, :] = (k[:, None] >= k[None, :])
    wmask[:, 2, :] = (k[:, None] <= k[None, :])
    return dict(ident=ident, tri_f=tri_f, tri_b=tri_b, rotT=rotT, cosE=cosE, sinE=sinE, wmask=wmask)


def _chunked(v):
    return np.ascontiguousarray(v.reshape(-1, 128).T)


def make_in_maps(x, c, ctx, c_ctx, mod_w, mod_b, norm_g, mlp_w1, mlp_w2, ab_w_in, a_w_s, a_b_s,
                 a_ln_g, a_ln_b, b_conv_w, b_conv_b, b_a_log, b_dt_bias, b_d, b_norm_g, ab_w_out,
                 cd_w_in, c_q_norm_g, c_k_norm_g, d_sink, cd_w_out):
    f = lambda a: np.ascontiguousarray(np.asarray(a, dtype=np.float32))
    shared = dict(_consts())
    shared["mod_w"] = f(mod_w)
    shared["mod_b"] = np.ascontiguousarray(f(mod_b).reshape(2, 96, 128).transpose(2, 0, 1))
    shared["norm_g"] = np.ascontiguousarray(f(norm_g).reshape(2, 4, 16, 128).transpose(3, 0, 1, 2))
    shared["mlp_w1"] = f(mlp_w1); shared["mlp_w2"] = f(mlp_w2)
    shared["ab_w_in"] = f(ab_w_in[0]); shared["ab_w_out"] = f(ab_w_out[0])
    shared["cd_w_in"] = f(cd_w_in[0]); shared["cd_w_out"] = f(cd_w_out[0])
    shared["wsT"] = np.ascontiguousarray(f(a_w_s[0]).transpose(2, 0, 1))
    shared["bs_row"] = np.ascontiguousarray(np.repeat(f(a_b_s[0]), 2, axis=0).reshape(1, 2048))
    shared["lng_row"] = f(a_ln_g[0]).reshape(1, 2048); shared["lnb_row"] = f(a_ln_b[0]).reshape(1, 2048)
    shared["conv_w"] = np.ascontiguousarray(f(b_conv_w[0]).reshape(5, 24, 128).transpose(2, 1, 0))
    shared["conv_b"] = np.ascontiguousarray(f(b_conv_b[0]).reshape(24, 128).T)
    shared["alog_row"] = f(b_a_log[0]).reshape(1, 64); shared["dtb_row"] = f(b_dt_bias[0]).reshape(1, 64)
    shared["bd"] = _chunked(np.repeat(f(b_d[0]), 64)); shared["gng"] = _chunked(f(b_norm_g[0]))
    shared["qg"] = f(c_q_norm_g[0]).reshape(128, 1); shared["kg"] = f(c_k_norm_g[0]).reshape(128, 1)
    shared["sink_row"] = f(d_sink[0]).reshape(1, 8)
    x = np.asarray(x); ctx = np.asarray(ctx); c = f(c); c_ctx = f(c_ctx)
    maps = []
    for core in range(NCORES):
        b0 = 2 * core
        hT = np.empty((D, NT), np.float32)
        for s in range(2):
            hT[:, s * SEQT:s * SEQT + CTXL] = ctx[b0 + s].T
            hT[:, s * SEQT + CTXL:(s + 1) * SEQT] = x[b0 + s].T
        cT = np.stack([c[b0], c[b0 + 1], c_ctx], axis=0)
        cT = np.ascontiguousarray(cT.reshape(3, 16, 128).transpose(2, 1, 0))
        m = dict(shared)
        m["hT"] = hT; m["cT"] = cT
        maps.append(m)
    return maps


_PROG_CACHE = {}


def get_prog(**kw):
    key = repr(sorted(kw.items()))
    if key not in _PROG_CACHE:
        p = Prog(**kw)
        p.build()
        _PROG_CACHE[key] = p
    return _PROG_CACHE[key]


def kernel(**inputs):
    p = get_prog()
    maps = make_in_maps(**inputs)
    res = run_bass_kernel_spmd(p.nc, maps, core_ids=list(range(NCORES)))
    out = np.empty((16, LAT, D), np.float32)
    for core in range(NCORES):
        oT = res.results[core]["outT"]
        for s in range(2):
            out[2 * core + s] = oT[:, s * LAT:(s + 1) * LAT].T
    return out
```

```python
import numpy as np
from contextlib import ExitStack
import concourse.bass as bass
import concourse.mybir as mybir
from concourse.bass_utils import run_bass_kernel_spmd

F32 = mybir.dt.float32
BF16 = mybir.dt.bfloat16
AF = mybir.ActivationFunctionType
ALU = mybir.AluOpType

NCORES = 8
D = 2048
DFF = 8192
SEQT = 2304
CTXL = 256
LAT = 2048
NT = 2 * SEQT
EPS = 1e-6
AB_IN = 9280
NH_B = 32


class Buf:
    __slots__ = ("t", "w", "r", "name")

    def __init__(self, t, name=None):
        self.t = t
        self.w = {}
        self.r = {}
        self.name = name

    def __getitem__(self, k):
        return self.t[k]


class Ctx:
    def __init__(self, nc, n_dma_sems=56):
        self.nc = nc
        self.E = {"pe": nc.tensor, "act": nc.scalar, "dve": nc.vector, "pool": nc.gpsimd, "sp": nc.sync}
        self.sem = {e: nc.alloc_semaphore("c_" + e) for e in ("pe", "act", "dve", "pool")}
        self.cnt = {e: 0 for e in self.sem}
        self.pend = {e: False for e in self.sem}
        self.waited = {e: {} for e in self.E}
        self.dsem = [nc.alloc_semaphore("d%d" % i) for i in range(n_dma_sems)]
        self.dval = [0] * n_dma_sems
        self.dpool = {"sp": list(range(0, 36)), "pool": list(range(36, n_dma_sems))}
        self.dnext = {"sp": 0, "pool": 0}
        self.bar = nc.alloc_semaphore("bar")
        self.barv = 0
        self.uid = 0
        self.stack = None

    def sb(self, shape, dtype, name="t"):
        self.uid += 1
        t = self.stack.enter_context(self.nc.sbuf_tensor("%s_%d" % (name, self.uid), list(shape), dtype))
        return Buf(t, name)

    def ps(self, shape=(128, 512), dtype=F32, name="p"):
        self.uid += 1
        t = self.stack.enter_context(self.nc.psum_tensor("%s_%d" % (name, self.uid), list(shape), dtype))
        return Buf(t, name)

    def _wait(self, eng, key, val):
        if key == ("c", "pe") and eng == "pe":
            return
        if self.waited[eng].get(key, 0) >= val:
            return
        sem = self.sem[key[1]] if key[0] == "c" else self.dsem[key[1]]
        self.E[eng].wait_ge(sem, val)
        self.waited[eng][key] = val

    def _deps(self, eng, reads, writes, join):
        for b in reads:
            for k, v in b.w.items():
                self._wait(eng, k, v)
        for b in writes:
            if not join:
                for k, v in b.w.items():
                    self._wait(eng, k, v)
            for k, v in b.r.items():
                self._wait(eng, k, v)

    def _note(self, key, val, reads, writes, join):
        for b in reads:
            if b.r.get(key, 0) < val:
                b.r[key] = val
        for b in writes:
            if join:
                b.w[key] = val
            else:
                b.w = {key: val}
                b.r = {}

    def op(self, eng, fn, reads=(), writes=(), inc=True, join=False):
        reads = [b for b in reads if b is not None]
        writes = [b for b in writes if b is not None]
        self._deps(eng, reads, writes, join)
        ins = fn(self.E[eng])
        if eng != "pe":
            inc = True
        if inc:
            self.cnt[eng] += 1
            ins.then_inc(self.sem[eng], 1)
            val = self.cnt[eng]
            self.pend[eng] = False
        else:
            val = self.cnt[eng] + 1
            self.pend[eng] = True
        self._note(("c", eng), val, reads, writes, join)
        return ins

    def dma(self, q, out_ap, in_ap, reads=(), writes=(), join=False, **kw):
        reads = [b for b in reads if b is not None]
        writes = [b for b in writes if b is not None]
        self._deps(q, reads, writes, join)
        pool = self.dpool[q]
        i = pool[self.dnext[q] % len(pool)]
        self.dnext[q] += 1
        if self.dval[i] > 0:
            self._wait(q, ("d", i), self.dval[i])
        self.dval[i] += 16
        ins = self.E[q].dma_start(out=out_ap, in_=in_ap, **kw)
        ins.then_inc(self.dsem[i], 16)
        self._note(("d", i), self.dval[i], reads, writes, join)
        return ins

    def barrier(self):
        assert not self.pend["pe"], "PE has an un-flushed milestone"
        for e in self.sem:
            if self.cnt[e] > 0:
                self._wait("sp", ("c", e), self.cnt[e])
        for i, v in enumerate(self.dval):
            if v > 0:
                self._wait("sp", ("d", i), v)
        self.barv += 1
        self.E["sp"].sem_inc(self.bar, 1)
        for e in ("pe", "act", "dve", "pool"):
            self.E[e].wait_ge(self.bar, self.barv)
            for e2 in self.sem:
                self.waited[e][("c", e2)] = self.cnt[e2]
            for i, v in enumerate(self.dval):
                self.waited[e][("d", i)] = v


def subs(W):
    n = (W + 511) // 512
    assert W % n == 0
    w = W // n
    return [(i * w, w) for i in range(n)]


def tiles_all():
    return [(s * SEQT + j * 768, 768) for s in range(2) for j in range(3)]


def tiles_lat():
    out = []
    for s in range(2):
        b = s * SEQT + CTXL
        out += [(b, 768), (b + 768, 768), (b + 1536, 512)]
    return out


def tile_segs(c0, W):
    s = c0 // SEQT
    off = c0 - s * SEQT
    if off < CTXL:
        return [(0, CTXL - off, 2), (CTXL - off, W, s)]
    return [(0, W, s)]


class Prog:
    def __init__(self, dbg=(), stop_after=None, layers=(0, 1)):
        self.nc = nc = bass.Bass("TRN2", target_bir_lowering=False)
        self.c = Ctx(nc)
        self.dbg = set(dbg)
        self.stop_after = stop_after
        self.layers = layers
        self.I = {}
        self.done = False
        self.wdone = {}

    def inp(self, name, shape, dt=F32):
        self.I[name] = self.nc.dram_tensor(name, list(shape), dt, kind="ExternalInput")
        return self.I[name]

    def dram(self, name, shape, dt):
        kind = "ExternalOutput" if name in self.dbg else "Internal"
        return self.nc.dram_tensor(name, list(shape), dt, kind=kind)

    def declare(self):
        inp = self.inp
        inp("hT", [D, NT]); inp("cT", [128, 16, 3])
        inp("mod_w", [2, D, 6 * D]); inp("mod_b", [128, 2, 96]); inp("norm_g", [128, 2, 4, 16])
        inp("mlp_w1", [2, D, DFF]); inp("mlp_w2", [2, DFF, D])
        inp("ab_w_in", [D, AB_IN]); inp("ab_w_out", [2 * D, D]); inp("cd_w_in", [D, 3072]); inp("cd_w_out", [D, D])
        inp("wsT", [128, 8, 128]); inp("bs_row", [1, 2048]); inp("lng_row", [1, 2048]); inp("lnb_row", [1, 2048])
        inp("conv_w", [128, 24, 5]); inp("conv_b", [128, 24])
        inp("alog_row", [1, 64]); inp("dtb_row", [1, 64]); inp("bd", [128, 16]); inp("gng", [128, 16])
        inp("qg", [128, 1]); inp("kg", [128, 1]); inp("sink_row", [1, 8])
        inp("ident", [128, 128]); inp("tri_f", [128, 128]); inp("tri_b", [128, 128]); inp("rotT", [128, 128])
        inp("cosE", [128, SEQT]); inp("sinE", [128, SEQT]); inp("wmask", [128, 3, 128])
        d = self.dram
        self.HA = d("HA", [D, NT], F32)
        self.HB = d("HB", [D, NT], F32)
        self.OUT = self.nc.dram_tensor("outT", [D, 2 * LAT], F32, kind="ExternalOutput")
        self.U = d("U", [D, NT], BF16)
        self.V = d("V", [NT, D], BF16)
        self.SZ = d("SZ", [D, NT], BF16)
        self.XBC = d("XBC", [3072, NT], F32)
        self.XC = d("XC", [3072, NT], F32)
        self.DTR = d("DTR", [NT, 64], F32)
        self.YF = d("YF", [D, NT], F32)
        self.YAB = d("YAB", [2 * D, NT], BF16)
        self.QKT = d("QKT", [20 * 128, NT], BF16)
        self.VT = d("VT", [NT, 512], BF16)
        self.W1B = d("W1B", [2, D, DFF], BF16); self.W2B = d("W2B", [2, DFF, D], BF16)
        self.ABIB = d("ABIB", [D, AB_IN], BF16); self.ABOB = d("ABOB", [2 * D, D], BF16)
        self.CDIB = d("CDIB", [D, 3072], BF16); self.CDOB = d("CDOB", [D, D], BF16)

    def persistent(self):
        c = self.c
        self.modT = c.sb([128, 2, 96, 3], F32, "modT")
        self.gsc = c.sb([128, 2, 2, 16, 3], F32, "gsc")
        self.gg = c.sb([128, 2, 2, 16, 3], F32, "gg")
        self.ng = c.sb([128, 2, 4, 16], F32, "ng")
        self.ones_bf = c.sb([128, 128], BF16, "ones")
        self.ones_f = c.sb([128, 128], F32, "onesf")
        self.eps_t = c.sb([128, 1], F32, "eps")
        c.op("dve", lambda e: e.memset(self.ones_bf[:], 1.0), writes=[self.ones_bf])
        c.op("dve", lambda e: e.memset(self.ones_f[:], 1.0), writes=[self.ones_f])
        c.op("dve", lambda e: e.memset(self.eps_t[:], EPS), writes=[self.eps_t])
        c.dma("sp", self.ng[:], self.I["norm_g"][:, :, :, :], writes=[self.ng])

    def end_phase(self, name):
        self.c.barrier()
        if self.stop_after == name:
            self.done = True
        return self.done

    def phase_mod(self):
        c, I = self.c, self.I
        with ExitStack() as st:
            c.stack = st
            cT = c.sb([128, 16, 3], F32); sbf = c.sb([128, 16, 3], BF16); modb = c.sb([128, 2, 96], F32)
            c.dma("sp", cT[:], I["cT"][:, :, :], writes=[cT])
            c.dma("sp", modb[:], I["mod_b"][:, :, :], writes=[modb])
            c.op("act", lambda e: e.activation(out=sbf[:], in_=cT[:], func=AF.Silu), reads=[cT], writes=[sbf])
            wbuf = [c.sb([128, 16, 1536], BF16, "modw") for _ in range(2)]
            ps = [c.ps() for _ in range(2)]
            it = 0
            for i in range(2):
                for cb in range(8):
                    wb = wbuf[it % 2]; p = ps[it % 2]; it += 1
                    for q in range(4):
                        c.dma("pool", wb[:, 4 * q:4 * q + 4, :],
                              I["mod_w"][i, 512 * q:512 * (q + 1), cb * 1536:(cb + 1) * 1536].rearrange("(kc p) n -> p kc n", p=128),
                              writes=[wb], join=(q > 0))
                    for oc in range(12):
                        for kc in range(16):
                            c.op("pe", lambda e: e.matmul(p[:, oc * 3:oc * 3 + 3], wb[:, kc, oc * 128:(oc + 1) * 128], sbf[:, kc, :],
                                                          start=(kc == 0), stop=(kc == 15)),
                                 reads=[wb, sbf], writes=[p], inc=(oc == 11 and kc == 15))
                    c.op("dve", lambda e: e.tensor_tensor(out=self.modT[:, i, cb * 12:(cb + 1) * 12, :],
                                                          in0=p[:, 0:36].rearrange("p (a b) -> p a b", b=3),
                                                          in1=modb[:, i, cb * 12:(cb + 1) * 12].unsqueeze(2).to_broadcast([128, 12, 3]),
                                                          op=ALU.add),
                         reads=[p, modb], writes=[self.modT], join=True)
            m, ng = self.modT, self.ng
            for i in range(2):
                for (lo, which, gi, dst) in ((16, 0, 0, self.gsc), (64, 1, 2, self.gsc)):
                    c.op("dve", lambda e: e.scalar_tensor_tensor(out=dst[:, i, which, :, :], in0=m[:, i, lo:lo + 16, :], scalar=1.0,
                                                                 in1=ng[:, i, gi, :].unsqueeze(2).to_broadcast([128, 16, 3]),
                                                                 op0=ALU.add, op1=ALU.mult),
                         reads=[m, ng], writes=[dst], join=True)
                for (lo, which, gi, dst) in ((32, 0, 1, self.gg), (80, 1, 3, self.gg)):
                    c.op("dve", lambda e: e.tensor_tensor(out=dst[:, i, which, :, :], in0=m[:, i, lo:lo + 16, :],
                                                          in1=ng[:, i, gi, :].unsqueeze(2).to_broadcast([128, 16, 3]), op=ALU.mult),
                         reads=[m, ng], writes=[dst], join=True)

    def shift_ap(self, i, which, kc, s):
        lo = 0 if which == 0 else 48
        return self.modT[:, i, lo + kc, s:s + 1]

    def norm_scratch(self, W):
        c = self.c
        S = {}
        S["hb"] = [c.sb([128, W], F32, "hb") for _ in range(3)]
        S["sq"] = [c.sb([128, W], BF16, "sq") for _ in range(2)]
        S["rstd"] = c.sb([128, W], F32, "rstd")
        S["tn"] = [c.sb([128, W], F32, "tn") for _ in range(2)]
        S["ps"] = [c.ps() for _ in range(2)]
        return S

    def rstd_from_ps(self, S, W, n):
        c = self.c
        for si, (lo, w) in enumerate(subs(W)):
            p = S["ps"][si]
            c.op("act", lambda e: e.activation(out=S["rstd"][:, lo:lo + w], in_=p[:, 0:w], func=AF.Sqrt,
                                               scale=1.0 / n, bias=self.eps_t[:, 0:1]),
                 reads=[p, self.eps_t], writes=[S["rstd"]], join=(si > 0))
        c.op("dve", lambda e: e.reciprocal(out=S["rstd"][:, 0:W], in_=S["rstd"][:, 0:W]), reads=[S["rstd"]], writes=[S["rstd"]])

    def load_modnorm(self, S, hsrc, c0, W, layer, which, aT):
        for _ in self.gen_modnorm(S, hsrc, c0, W, layer, which, aT):
            pass

    def gen_modnorm(self, S, hsrc, c0, W, layer, which, aT):
        c = self.c
        sb = subs(W)
        for kc in range(16):
            hb = S["hb"][kc % 3]
            c.dma("sp", hb[:, 0:W], hsrc[kc * 128:(kc + 1) * 128, c0:c0 + W], writes=[hb])
            sq = S["sq"][kc % 2]
            c.op("act", lambda e: e.activation(out=sq[:, 0:W], in_=hb[:, 0:W], func=AF.Square), reads=[hb], writes=[sq])
            for si, (lo, w) in enumerate(sb):
                p = S["ps"][si]
                c.op("pe", lambda e: e.matmul(p[:, 0:w], self.ones_bf[:, :], sq[:, lo:lo + w], start=(kc == 0), stop=(kc == 15)),
                     reads=[sq, self.ones_bf], writes=[p], inc=(si == len(sb) - 1))
            yield
        self.rstd_from_ps(S, W, float(D))
        yield
        segs = tile_segs(c0, W)
        for kc in range(16):
            hb = S["hb"][kc % 3]
            c.dma("sp", hb[:, 0:W], hsrc[kc * 128:(kc + 1) * 128, c0:c0 + W], writes=[hb])
            t = S["tn"][kc % 2]
            c.op("dve", lambda e: e.tensor_tensor(out=t[:, 0:W], in0=hb[:, 0:W], in1=S["rstd"][:, 0:W], op=ALU.mult),
                 reads=[hb, S["rstd"]], writes=[t])
            for gi, (lo, hi, s) in enumerate(segs):
                c.op("act", lambda e: e.activation(out=aT[:, kc, lo:hi], in_=t[:, lo:hi], func=AF.Identity,
                                                   scale=self.gsc[:, layer, which, kc, s:s + 1], bias=self.shift_ap(layer, which, kc, s)),
                     reads=[t, self.gsc, self.modT], writes=[aT], join=True)
            yield

    def load_w(self, wb, src2d, KC, ncols, cache=None, key=None, col0=0):
        c = self.c
        step = 4
        if cache is not None and key in self.wdone:
            trk = self.wdone[key]
            for k0 in range(0, KC, step):
                c.dma("pool", wb[:, k0:k0 + step, col0:col0 + ncols],
                      cache[k0 * 128:(k0 + step) * 128, :].rearrange("(kc p) n -> p kc n", p=128),
                      reads=[trk], writes=[wb], join=(k0 > 0 or col0 > 0))
            return
        for k0 in range(0, KC, step):
            c.dma("pool", wb[:, k0:k0 + step, col0:col0 + ncols],
                  src2d[k0 * 128:(k0 + step) * 128, :].rearrange("(kc p) n -> p kc n", p=128),
                  writes=[wb], join=(k0 > 0 or col0 > 0))
        if cache is not None:
            trk = Buf(None, "wtrk")
            for k0 in range(0, KC, step):
                c.dma("sp", cache[k0 * 128:(k0 + step) * 128, :].rearrange("(kc p) n -> p kc n", p=128),
                      wb[:, k0:k0 + step, col0:col0 + ncols], reads=[wb], writes=[trk], join=True)
            self.wdone[key] = trk

    def post_residual(self, S, yacc, W, hsrc, c0, layer, which, hdst, d0):
        for _ in self.gen_post_residual(S, yacc, W, hsrc, c0, layer, which, hdst, d0):
            pass

    def gen_post_residual(self, S, yacc, W, hsrc, c0, layer, which, hdst, d0):
        c = self.c
        sb = subs(W)
        for kc in range(16):
            sq = S["sq"][kc % 2]
            c.op("act", lambda e: e.activation(out=sq[:, 0:W], in_=yacc[:, kc, 0:W], func=AF.Square), reads=[yacc], writes=[sq])
            for si, (lo, w) in enumerate(sb):
                p = S["ps"][si]
                c.op("pe", lambda e: e.matmul(p[:, 0:w], self.ones_bf[:, :], sq[:, lo:lo + w], start=(kc == 0), stop=(kc == 15)),
                     reads=[sq, self.ones_bf], writes=[p], inc=(si == len(sb) - 1))
            yield
        self.rstd_from_ps(S, W, float(D))
        yield
        segs = tile_segs(c0, W)
        for kc in range(16):
            hb = S["hb"][kc % 3]
            c.dma("sp", hb[:, 0:W], hsrc[kc * 128:(kc + 1) * 128, c0:c0 + W], writes=[hb])
            t = S["tn"][kc % 2]
            c.op("dve", lambda e: e.tensor_tensor(out=t[:, 0:W], in0=yacc[:, kc, 0:W], in1=S["rstd"][:, 0:W], op=ALU.mult),
                 reads=[yacc, S["rstd"]], writes=[t])
            for (lo, hi, s) in segs:
                c.op("dve", lambda e: e.scalar_tensor_tensor(out=hb[:, lo:hi], in0=t[:, lo:hi], scalar=self.gg[:, layer, which, kc, s:s + 1],
                                                              in1=hb[:, lo:hi], op0=ALU.mult, op1=ALU.add),
                     reads=[t, self.gg, hb], writes=[hb], join=True)
            c.dma("sp", hdst[kc * 128:(kc + 1) * 128, d0:d0 + W], hb[:, 0:W], reads=[hb])
            yield

    def phase_mlp(self, layer, hsrc, hdst, tiles, dst_map):
        c, I = self.c, self.I
        w1 = I["mlp_w1"]; w2 = I["mlp_w2"]
        WT = 256
        NJ = 2048 // WT
        NO = WT // 128
        with ExitStack() as st:
            c.stack = st
            WMAX = 768
            S = self.norm_scratch(WMAX)
            aTs = [c.sb([128, 16, WMAX], BF16, "aT") for _ in range(2)]
            yacc = c.sb([128, 16, WMAX], F32, "yacc")
            h1 = c.sb([128, 16, WMAX], BF16, "h1")
            w1b = [c.sb([128, 16, WT], BF16, "w1b") for _ in range(3)]
            w2b = [c.sb([128, 16, WT], BF16, "w2b") for _ in range(3)]
            sqt = [c.sb([128, 512], F32, "sqt") for _ in range(2)]
            pp = [c.ps() for _ in range(6)]
            pi = 0
            self.load_modnorm(S, hsrc, tiles[0][0], tiles[0][1], layer, 1, aTs[0])
            pend_post = None
            for ti, (c0, W) in enumerate(tiles):
                aT = aTs[ti % 2]
                pend_norm = None
                if ti + 1 < len(tiles):
                    pend_norm = self.gen_modnorm(S, hsrc, tiles[ti + 1][0], tiles[ti + 1][1], layer, 1, aTs[(ti + 1) % 2])
                sb = subs(W)
                sched = []
                for fb in range(4):
                    sched += [("w1", fb, j) for j in range(NJ)] + [("w2", fb, j) for j in range(NJ)]

                def bufof(idx):
                    kind, fb, j = sched[idx]
                    return (w1b if kind == "w1" else w2b)[(fb * NJ + j) % 3]

                def issue(idx):
                    kind, fb, j = sched[idx]
                    if kind == "w1":
                        self.load_w(bufof(idx), w1[layer, :, fb * 2048 + j * WT: fb * 2048 + (j + 1) * WT], 16, WT,
                                    cache=self.W1B[layer, :, fb * 2048 + j * WT: fb * 2048 + (j + 1) * WT], key=("w1", layer, fb, j))
                    else:
                        self.load_w(bufof(idx), w2[layer, fb * 2048:(fb + 1) * 2048, j * WT:(j + 1) * WT], 16, WT,
                                    cache=self.W2B[layer, fb * 2048:(fb + 1) * 2048, j * WT:(j + 1) * WT], key=("w2", layer, fb, j))

                issue(0); issue(1)
                for idx, (kind, fb, j) in enumerate(sched):
                    if idx + 2 < len(sched):
                        issue(idx + 2)
                    if kind == "w2" and fb == 0 and j == 0 and pend_post is not None:
                        for _ in pend_post:
                            pass
                        pend_post = None
                    wb = bufof(idx)
                    for oc in range(NO):
                        och = j * NO + oc
                        for (lo, w) in sb:
                            if fb == 0 and kind == "w1" and pend_post is not None:
                                next(pend_post, None); next(pend_post, None)
                            elif fb >= 1 and pend_norm is not None:
                                next(pend_norm, None)
                            p = pp[pi % 6]; pi += 1
                            if kind == "w1":
                                for kc in range(16):
                                    c.op("pe", lambda e: e.matmul(p[:, 0:w], wb[:, kc, oc * 128:(oc + 1) * 128], aT[:, kc, lo:lo + w],
                                                                  start=(kc == 0), stop=(kc == 15)),
                                         reads=[wb, aT], writes=[p], inc=(kc == 15))
                                sq = sqt[pi % 2]
                                c.op("act", lambda e: e.activation(out=sq[:, 0:w], in_=p[:, 0:w], func=AF.Square), reads=[p], writes=[sq])
                                c.op("dve", lambda e: e.scalar_tensor_tensor(out=h1[:, och, lo:lo + w], in0=p[:, 0:w], scalar=0.0, in1=sq[:, 0:w],
                                                                             op0=ALU.is_gt, op1=ALU.mult),
                                     reads=[p, sq], writes=[h1], join=True)
                            else:
                                for kc in range(16):
                                    c.op("pe", lambda e: e.matmul(p[:, 0:w], wb[:, kc, oc * 128:(oc + 1) * 128], h1[:, kc, lo:lo + w],
                                                                  start=(kc == 0), stop=(kc == 15)),
                                         reads=[wb, h1], writes=[p], inc=(kc == 15))
                                if fb == 0:
                                    c.op("act", lambda e: e.activation(out=yacc[:, och, lo:lo + w], in_=p[:, 0:w], func=AF.Copy),
                                         reads=[p], writes=[yacc], join=True)
                                else:
                                    c.op("dve", lambda e: e.tensor_tensor(out=yacc[:, och, lo:lo + w], in0=p[:, 0:w], in1=yacc[:, och, lo:lo + w], op=ALU.add),
                                         reads=[p, yacc], writes=[yacc], join=True)
                if pend_norm is not None:
                    for _ in pend_norm:
                        pass
                if pend_post is not None:
                    for _ in pend_post:
                        pass
                pend_post = self.gen_post_residual(S, yacc, W, hsrc, c0, layer, 1, hdst, dst_map(c0))
            for _ in pend_post:
                pass

    def phase_outproj(self, layer, wout, KC, hsrc, hdst, tiles):
        c = self.c
        with ExitStack() as st:
            c.stack = st
            WMAX = 768
            S = self.norm_scratch(WMAX)
            src = c.sb([128, KC, WMAX], BF16, "src")
            yaccs = [c.sb([128, 16, WMAX], F32, "yacc") for _ in range(2)]
            wbs = [c.sb([128, KC, 256], BF16, "wo") for _ in range(2)]
            pp = [c.ps() for _ in range(6)]
            pi = 0
            pend_post = None
            wcache = self.ABOB if layer == 0 else self.CDOB
            for ti, (c0, W) in enumerate(tiles):
                yacc = yaccs[ti % 2]
                for k0 in range(0, KC, 4):
                    c.dma("sp", src[:, k0:k0 + 4, 0:W], self.YAB[k0 * 128:(k0 + 4) * 128, c0:c0 + W].rearrange("(kc p) n -> p kc n", p=128),
                          writes=[src], join=(k0 > 0))
                sb = subs(W)
                self.load_w(wbs[0], wout[:, 0:256], KC, 256, cache=wcache[:, 0:256], key=("wo", layer, 0))
                for j in range(8):
                    if j + 1 < 8:
                        self.load_w(wbs[(j + 1) % 2], wout[:, (j + 1) * 256:(j + 2) * 256], KC, 256,
                                    cache=wcache[:, (j + 1) * 256:(j + 2) * 256], key=("wo", layer, j + 1))
                    wb = wbs[j % 2]
                    for oc in range(2):
                        ob = j * 2 + oc
                        for (lo, w) in sb:
                            if pend_post is not None:
                                next(pend_post, None); next(pend_post, None)
                            p = pp[pi % 6]; pi += 1
                            for kc in range(KC):
                                c.op("pe", lambda e: e.matmul(p[:, 0:w], wb[:, kc, oc * 128:(oc + 1) * 128], src[:, kc, lo:lo + w],
                                                              start=(kc == 0), stop=(kc == KC - 1)),
                                     reads=[wb, src], writes=[p], inc=(kc == KC - 1))
                            c.op("act", lambda e: e.activation(out=yacc[:, ob, lo:lo + w], in_=p[:, 0:w], func=AF.Copy),
                                 reads=[p], writes=[yacc], join=True)
                if pend_post is not None:
                    for _ in pend_post:
                        pass
                pend_post = self.gen_post_residual(S, yacc, W, hsrc, c0, layer, 0, hdst, c0)
            for _ in pend_post:
                pass

    def phase_cd_inproj(self, hsrc):
        c, I = self.c, self.I
        wi = I["cd_w_in"]
        with ExitStack() as st:
            c.stack = st
            W = 768
            S = self.norm_scratch(W)
            aT = c.sb([128, 16, W], BF16, "aT")
            wbs = [c.sb([128, 16, 512], BF16, "wi") for _ in range(2)]
            cosE = c.sb([128, SEQT], F32, "cos"); sinE = c.sb([128, SEQT], F32, "sin")
            rotT = c.sb([128, 128], F32, "rot"); qg = c.sb([128, 1], F32); kg = c.sb([128, 1], F32)
            c.dma("sp", cosE[:], I["cosE"][:, :], writes=[cosE]); c.dma("sp", sinE[:], I["sinE"][:, :], writes=[sinE])
            c.dma("sp", rotT[:], I["rotT"][:, :], writes=[rotT])
            c.dma("sp", qg[:], I["qg"][:, :], writes=[qg]); c.dma("sp", kg[:], I["kg"][:, :], writes=[kg])
            xs = [c.sb([128, 384], F32, "xs") for _ in range(3)]
            xn = [c.sb([128, 384], F32, "xn") for _ in range(3)]
            sqb = [c.sb([128, 384], BF16, "sqb") for _ in range(3)]
            rs = [c.sb([128, 384], F32, "rs") for _ in range(3)]
            t1 = [c.sb([128, 384], F32, "t1") for _ in range(3)]
            t2 = [c.sb([128, 384], F32, "t2") for _ in range(3)]
            ob = [c.sb([128, W], BF16, "ob") for _ in range(2)]
            vob = [c.sb([128, 6, 256], BF16, "vob") for _ in range(2)]
            pg = [c.ps() for _ in range(3)]
            pn = c.ps(); pr = c.ps(); pv = c.ps()
            plan = []
            for t in range(6):
                if t in (0, 1):
                    plan.append([("C", qg, t * 4 + o) for o in range(4)])
                elif t == 2:
                    plan.append([("C", kg, 8), ("C", kg, 9), ("V", None, 0)])
                elif t in (3, 4):
                    plan.append([("D", None, 10 + (t - 3) * 4 + o) for o in range(4)])
                else:
                    plan.append([("D", None, 18), ("D", None, 19), ("V", None, 256)])
            ei = 0
            oc_i = [0]
            stage2 = [None]; stage3 = [None]
            for (c0, _) in tiles_all():
                self.load_modnorm(S, hsrc, c0, W, 1, 0, aT)
                sq0 = c0 - (c0 // SEQT) * SEQT
                sb = subs(W)
                self.load_w(wbs[0], wi[:, 0:512], 16, 512, cache=self.CDIB[:, 0:512], key=("cdi", 0))
                for t in range(6):
                    if t + 1 < 6:
                        self.load_w(wbs[(t + 1) % 2], wi[:, (t + 1) * 512:(t + 2) * 512], 16, 512,
                                    cache=self.CDIB[:, (t + 1) * 512:(t + 2) * 512], key=("cdi", t + 1))
                    wb = wbs[t % 2]
                    for oi, (kind, gv, row) in enumerate(plan[t]):
                        if kind == "V":
                            vo = vob[(t // 3) % 2]
                            for tb in range(6):
                                for kc in range(16):
                                    c.op("pe", lambda e: e.matmul(pv[:, 0:256], aT[:, kc, tb * 128:(tb + 1) * 128], wb[:, kc, 256:512],
                                                                  start=(kc == 0), stop=(kc == 15)),
                                         reads=[wb, aT], writes=[pv], inc=(kc == 15))
                                c.op("act", lambda e: e.activation(out=vo[:, tb, :], in_=pv[:, 0:256], func=AF.Copy), reads=[pv], writes=[vo], join=True)
                            c.dma("sp", self.VT[c0:c0 + W, row:row + 256].rearrange("(n p) d -> p n d", p=128), vo[:, :, :], reads=[vo])
                            continue
                        o = ob[oc_i[0] % 2]; oc_i[0] += 1
                        for si, (lo, w) in enumerate(sb):
                            k = ei % 3; ei += 1
                            p = pg[k]
                            for kc in range(16):
                                c.op("pe", lambda e: e.matmul(p[:, 0:w], wb[:, kc, oi * 128:(oi + 1) * 128], aT[:, kc, lo:lo + w],
                                                              start=(kc == 0), stop=(kc == 15)),
                                     reads=[wb, aT], writes=[p], inc=(kc == 15))
                            x = xs[k]
                            c.op("act", lambda e: e.activation(out=x[:, 0:w], in_=p[:, 0:w], func=AF.Copy), reads=[p], writes=[x])
                            if kind == "C":
                                c.op("act", lambda e: e.activation(out=sqb[k][:, 0:w], in_=x[:, 0:w], func=AF.Square), reads=[x], writes=[sqb[k]])
                            if stage3[0] is not None:
                                stage3[0](); stage3[0] = None
                            if stage2[0] is not None:
                                stage3[0] = stage2[0](); stage2[0] = None
                            stage2[0] = self._cd_stage2(kind, gv, k, w, lo, sq0, o, row, c0, W, si == len(sb) - 1,
                                                        xs, xn, sqb, rs, t1, t2, pn, pr, rotT, cosE, sinE)
                if stage2[0] is not None:
                    if stage3[0] is not None:
                        stage3[0](); stage3[0] = None
                    stage3[0] = stage2[0](); stage2[0] = None
                if stage3[0] is not None:
                    stage3[0](); stage3[0] = None

    def _cd_stage2(self, kind, gv, k, w, lo, sq0, o, row, c0, W, last, xs, xn, sqb, rs, t1, t2, pn, pr, rotT, cosE, sinE):
        c = self.c
        x = xs[k]

        def stage3(y):
            c.op("pe", lambda e: e.matmul(pr[:, 0:w], rotT[:, :], y[:, 0:w], start=True, stop=True), reads=[rotT, y], writes=[pr])
            cl = sq0 + lo
            c.op("pool", lambda e: e.tensor_tensor(out=t1[k][:, 0:w], in0=y[:, 0:w], in1=cosE[:, cl:cl + w], op=ALU.mult),
                 reads=[y, cosE], writes=[t1[k]])
            c.op("dve", lambda e: e.tensor_tensor(out=t2[k][:, 0:w], in0=pr[:, 0:w], in1=sinE[:, cl:cl + w], op=ALU.mult),
                 reads=[pr, sinE], writes=[t2[k]])
            c.op("dve", lambda e: e.tensor_tensor(out=o[:, lo:lo + w], in0=t1[k][:, 0:w], in1=t2[k][:, 0:w], op=ALU.add),
                 reads=[t1[k], t2[k]], writes=[o], join=True)
            if last:
                c.dma("sp", self.QKT[row * 128:(row + 1) * 128, c0:c0 + W], o[:, 0:W], reads=[o])

        def stage2():
            if kind == "C":
                c.op("pe", lambda e: e.matmul(pn[:, 0:w], self.ones_bf[:, :], sqb[k][:, 0:w], start=True, stop=True),
                     reads=[sqb[k], self.ones_bf], writes=[pn])
                c.op("act", lambda e: e.activation(out=rs[k][:, 0:w], in_=pn[:, 0:w], func=AF.Sqrt, scale=1.0 / 128, bias=self.eps_t[:, 0:1]),
                     reads=[pn, self.eps_t], writes=[rs[k]])
                c.op("dve", lambda e: e.reciprocal(out=rs[k][:, 0:w], in_=rs[k][:, 0:w]), reads=[rs[k]], writes=[rs[k]])
                c.op("dve", lambda e: e.scalar_tensor_tensor(out=xn[k][:, 0:w], in0=x[:, 0:w], scalar=gv[:, 0:1], in1=rs[k][:, 0:w],
                                                             op0=ALU.mult, op1=ALU.mult),
                     reads=[x, gv, rs[k]], writes=[xn[k]])
                y = xn[k]
            else:
                y = x
            return lambda: stage3(y)
        return stage2

    def phase_attn(self):
        c, I = self.c, self.I
        SC = 128 ** -0.5
        with ExitStack() as st:
            c.stack = st
            KT = [c.sb([128, SEQT], BF16, "KT") for _ in range(2)]
            Vt = [c.sb([128, 18, 128], BF16, "Vt") for _ in range(2)]
            QT = [c.sb([128, LAT], BF16, "QT") for _ in range(2)]
            PT = [c.sb([128, 640], BF16, "PT") for _ in range(3)]
            Pf = [c.sb([128, 384], F32, "Pf") for _ in range(2)]
            rden = [c.sb([128, 512], F32, "rden") for _ in range(2)]
            oo = [c.sb([128, 512], BF16, "oo") for _ in range(2)]
            wm = c.sb([128, 3, 128], F32, "wm")
            snk = c.sb([128, 8], F32, "snk"); esk = c.sb([128, 8], F32, "esk")
            c.dma("sp", wm[:], I["wmask"][:, :, :], writes=[wm])
            c.dma("sp", snk[:], I["sink_row"][0:1, :].partition_broadcast(128), writes=[snk])
            c.op("act", lambda e: e.activation(out=esk[:], in_=snk[:], func=AF.Exp), reads=[snk], writes=[esk])
            pS = [c.ps() for _ in range(2)]
            pSb = [c.ps() for _ in range(2)]
            pO = [c.ps() for _ in range(2)]
            pD = [c.ps() for _ in range(2)]
            li = 0
            qi = 0
            ui = 0
            pti = 0
            for mode in ("C", "D"):
                krow0 = 8 if mode == "C" else 18
                qrow0 = 0 if mode == "C" else 10
                vcol0 = 0 if mode == "C" else 256
                orow0 = 0 if mode == "C" else 8
                for s in range(2):
                    for kh in range(2):
                        kt = KT[li % 2]; vt = Vt[li % 2]; li += 1
                        c.dma("sp", kt[:], self.QKT[(krow0 + kh) * 128:(krow0 + kh + 1) * 128, s * SEQT:(s + 1) * SEQT], writes=[kt])
                        c.dma("sp", vt[:], self.VT[s * SEQT:(s + 1) * SEQT, vcol0 + kh * 128: vcol0 + (kh + 1) * 128].rearrange("(n p) d -> p n d", p=128),
                              writes=[vt])
                        for qh in range(4):
                            h = kh * 4 + qh
                            qt = QT[qi % 2]; qi += 1
                            c.dma("sp", qt[:], self.QKT[(qrow0 + h) * 128:(qrow0 + h + 1) * 128, s * SEQT + CTXL:(s + 1) * SEQT], writes=[qt])
                            for qs in range(4):
                                po = pO[ui % 2]; pd = pD[ui % 2]; ui += 1
                                if mode == "C":
                                    def smm(kc):
                                        p = pS[kc % 2]
                                        c.op("pe", lambda e: e.matmul(p[:, 0:512], kt[:, kc * 128:(kc + 1) * 128], qt[:, qs * 512:(qs + 1) * 512],
                                                                      start=True, stop=True), reads=[kt, qt], writes=[p])
                                    smm(0)
                                    for kc in range(18):
                                        if kc + 1 < 18:
                                            smm(kc + 1)
                                        p = pS[kc % 2]
                                        pt = PT[pti % 3]; pti += 1
                                        c.op("act", lambda e: e.activation(out=pt[:, 0:512], in_=p[:, 0:512], func=AF.Exp, scale=SC), reads=[p], writes=[pt])
                                        c.op("pe", lambda e: e.matmul(po[:, 0:512], vt[:, kc, :], pt[:, 0:512], start=(kc == 0), stop=(kc == 17)),
                                             reads=[vt, pt], writes=[po], inc=False)
                                        c.op("pe", lambda e: e.matmul(pd[:, 0:512], self.ones_bf[:, :], pt[:, 0:512], start=(kc == 0), stop=(kc == 17)),
                                             reads=[self.ones_bf, pt], writes=[pd], inc=True)
                                    rd = rden[ui % 2]
                                    c.op("dve", lambda e: e.reciprocal(out=rd[:, :], in_=pd[:, 0:512]), reads=[pd], writes=[rd])
                                else:
                                    def d_scores(qq):
                                        qb = qs * 4 + qq
                                        kcs = [k for k in (qb - 1, qb, qb + 1) if 0 <= k < 16]
                                        pa = pS[qb % 2]; pb = pSb[qb % 2]
                                        qsl = qt[:, qb * 128:(qb + 1) * 128]
                                        for k in kcs:
                                            j = k - (qb - 1)
                                            c.op("pe", lambda e: e.matmul(pa[:, j * 128:(j + 1) * 128], kt[:, (2 + k) * 128:(3 + k) * 128], qsl, start=True, stop=True),
                                                 reads=[kt, qt], writes=[pa], inc=False)
                                        for j in range(2):
                                            c.op("pe", lambda e: e.matmul(pb[:, j * 128:(j + 1) * 128], kt[:, j * 128:(j + 1) * 128], qsl, start=True, stop=True),
                                                 reads=[kt, qt], writes=[pb], inc=(j == 1))

                                    def d_rest(qq, pt):
                                        qb = qs * 4 + qq
                                        kcs = [k for k in (qb - 1, qb, qb + 1) if 0 <= k < 16]
                                        j0 = kcs[0] - (qb - 1)
                                        pa = pS[qb % 2]; pb = pSb[qb % 2]
                                        nl = len(kcs)
                                        pf = Pf[qb % 2]
                                        c.op("act", lambda e: e.activation(out=pf[:, 0:nl * 128], in_=pa[:, j0 * 128:(j0 + nl) * 128], func=AF.Exp, scale=SC),
                                             reads=[pa], writes=[pf])
                                        c.op("act", lambda e: e.activation(out=pt[:, 384:640], in_=pb[:, 0:256], func=AF.Exp, scale=SC),
                                             reads=[pb], writes=[pt], join=True)
                                        c.op("dve", lambda e: e.tensor_tensor(out=pt[:, 0:nl * 128].rearrange("p (a b) -> p a b", b=128),
                                                                              in0=pf[:, 0:nl * 128].rearrange("p (a b) -> p a b", b=128),
                                                                              in1=wm[:, j0:j0 + nl, :], op=ALU.mult),
                                             reads=[pf, wm], writes=[pt], join=True)
                                        ops = [(vt[:, 2 + k, :], pt[:, (k - kcs[0]) * 128:(k - kcs[0] + 1) * 128]) for k in kcs]
                                        ops += [(vt[:, j, :], pt[:, 384 + j * 128:384 + (j + 1) * 128]) for j in range(2)]
                                        n = len(ops)
                                        for i2, (va, pa2) in enumerate(ops):
                                            c.op("pe", lambda e: e.matmul(po[:, qq * 128:(qq + 1) * 128], va, pa2, start=(i2 == 0), stop=(i2 == n - 1)),
                                                 reads=[vt, pt], writes=[po], inc=False)
                                        for i2, (va, pa2) in enumerate(ops):
                                            c.op("pe", lambda e: e.matmul(pd[:, qq * 128:(qq + 1) * 128], self.ones_bf[:, :], pa2, start=(i2 == 0), stop=(i2 == n - 1)),
                                                 reads=[self.ones_bf, pt], writes=[pd], inc=(i2 == n - 1))

                                    d_scores(0)
                                    for qq in range(4):
                                        if qq + 1 < 4:
                                            d_scores(qq + 1)
                                        pt_ = PT[pti % 3]; pti += 1
                                        d_rest(qq, pt_)
                                    rd = rden[ui % 2]
                                    c.op("dve", lambda e: e.tensor_scalar(out=rd[:, :], in0=pd[:, 0:512], scalar1=esk[:, h:h + 1], scalar2=None, op0=ALU.add),
                                         reads=[pd, esk], writes=[rd])
                                    c.op("dve", lambda e: e.reciprocal(out=rd[:, :], in_=rd[:, :]), reads=[rd], writes=[rd])
                                o = oo[ui % 2]
                                c.op("dve", lambda e: e.tensor_tensor(out=o[:, :], in0=po[:, 0:512], in1=rd[:, :], op=ALU.mult), reads=[po, rd], writes=[o])
                                c0 = s * SEQT + CTXL + qs * 512
                                c.dma("sp", self.YAB[(orow0 + h) * 128:(orow0 + h + 1) * 128, c0:c0 + 512], o[:, :], reads=[o])

    def build(self):
        c = self.c
        self.declare()
        with ExitStack() as gst:
            c.stack = gst
            self.persistent()
            self._run()
            if not self.done:
                c.barrier()
        return self.nc

    def _run(self):
        I = self.I
        self.phase_mod()
        if "MODT" in self.dbg:
            t = self.nc.dram_tensor("MODT", [128, 2 * 96 * 3], F32, kind="ExternalOutput")
            self.c.dma("sp", t[:, :], self.modT[:].rearrange("p a b c -> p (a b c)"), reads=[self.modT])
            t2 = self.nc.dram_tensor("GSC", [128, 2 * 2 * 16 * 3], F32, kind="ExternalOutput")
            self.c.dma("sp", t2[:, :], self.gsc[:].rearrange("p a b c d -> p (a b c d)"), reads=[self.gsc])
            t3 = self.nc.dram_tensor("GG", [128, 2 * 2 * 16 * 3], F32, kind="ExternalOutput")
            self.c.dma("sp", t3[:, :], self.gg[:].rearrange("p a b c d -> p (a b c d)"), reads=[self.gg])
        if self.end_phase("mod"):
            return
        h0 = I["hT"]
        if 0 in self.layers:
            from_l0 = self.layer0(h0)
            if self.done:
                return
            h_in = self.HB
        else:
            h_in = h0
        if 1 in self.layers:
            self.phase_cd_inproj(h_in)
            if self.end_phase("cd_in"):
                return
            self.phase_attn()
            if self.end_phase("attn"):
                return
            self.phase_outproj(1, I["cd_w_out"], 16, h_in, self.HA, tiles_lat())
            if self.end_phase("cd_out"):
                return
            self.phase_mlp(1, self.HA, self.OUT, tiles_lat(), lambda c0: c0 - CTXL * (c0 // SEQT + 1))
            if self.end_phase("mlp1"):
                return

    def phase_ab_inproj(self, hsrc):
        c, I = self.c, self.I
        wi = I["ab_w_in"]
        AXX = mybir.AxisListType.X
        with ExitStack() as st:
            c.stack = st
            W = 768
            S = self.norm_scratch(W)
            aTs = [c.sb([128, 16, W], BF16, "aT") for _ in range(2)]
            cur = [aTs[0]]
            Wv = c.sb([128, 16, 2048], BF16, "Wv")
            Wdt = c.sb([128, 16, 64], BF16, "Wdt")
            wbs = [c.sb([128, 16, 256], BF16, "wi") for _ in range(2)]
            lng = c.sb([128, 2048], F32, "lng"); lnb = c.sb([128, 2048], F32, "lnb")
            c.dma("sp", lng[:], I["lng_row"][0:1, :].partition_broadcast(128), writes=[lng])
            c.dma("sp", lnb[:], I["lnb_row"][0:1, :].partition_broadcast(128), writes=[lnb])
            for q in range(4):
                self.load_w_cols(Wv, q * 512, wi[:, 2048 + q * 512: 2048 + (q + 1) * 512], 16, 512, first=(q == 0))
            self.load_w(Wdt, wi[:, 9216:9280], 16, 64)
            ob = [c.sb([128, W], BF16, "ob") for _ in range(2)]
            obf = [c.sb([128, W], F32, "obf") for _ in range(2)]
            vg = c.sb([128, 2048], F32, "vg"); vn = c.sb([128, 2048], F32, "vn"); vbf = c.sb([128, 2048], BF16, "vbf")
            st4 = c.sb([128, 8], F32, "st4")
            dto = c.sb([128, 6, 64], F32, "dto")
            pg = [c.ps() for _ in range(2)]
            pv = [c.ps() for _ in range(4)]
            ftiles = [(q * 256, "u", self.U, q * 256) for q in range(8)]
            ftiles += [(4096 + q * 256, "z", self.SZ, q * 256) for q in range(8)]
            ftiles += [(6144 + q * 256, "x", self.XBC, q * 256) for q in range(12)]
            st_ei = [0, 0]
            def vblock(tb, c0):
                aT = cur[0]
                asl = lambda kc: aT[:, kc, tb * 128:(tb + 1) * 128]
                for cb in range(4):
                    for kc in range(16):
                        c.op("pe", lambda e: e.matmul(pv[cb][:, 0:512], asl(kc), Wv[:, kc, cb * 512:(cb + 1) * 512],
                                                      start=(kc == 0), stop=(kc == 15)),
                             reads=[Wv, aT], writes=[pv[cb]], inc=(kc == 15))
                    c.op("act", lambda e: e.activation(out=vg[:, cb * 512:(cb + 1) * 512], in_=pv[cb][:, 0:512], func=AF.Gelu),
                         reads=[pv[cb]], writes=[vg], join=(cb > 0))
                p = pg[st_ei[0] % 2]; st_ei[0] += 1
                for kc in range(16):
                    c.op("pe", lambda e: e.matmul(p[:, 0:64], asl(kc), Wdt[:, kc, :], start=(kc == 0), stop=(kc == 15)),
                         reads=[Wdt, aT], writes=[p], inc=(kc == 15))
                c.op("act", lambda e: e.activation(out=dto[:, tb, :], in_=p[:, 0:64], func=AF.Copy), reads=[p], writes=[dto], join=(tb > 0))
                c.op("dve", lambda e: e.reduce_sum(out=st4[:, 0:1], in_=vg[:, :], axis=AXX), reads=[vg], writes=[st4])
                c.op("act", lambda e: e.activation(out=vn[:, :], in_=vg[:, :], func=AF.Square), reads=[vg], writes=[vn])
                c.op("dve", lambda e: e.reduce_sum(out=st4[:, 1:2], in_=vn[:, :], axis=AXX), reads=[vn], writes=[st4])
                c.op("dve", lambda e: e.tensor_scalar(out=st4[:, 2:3], in0=st4[:, 0:1], scalar1=1.0 / 2048, scalar2=None, op0=ALU.mult),
                     reads=[st4], writes=[st4])
                c.op("dve", lambda e: e.tensor_tensor(out=st4[:, 3:4], in0=st4[:, 2:3], in1=st4[:, 2:3], op=ALU.mult), reads=[st4], writes=[st4])
                c.op("dve", lambda e: e.scalar_tensor_tensor(out=st4[:, 4:5], in0=st4[:, 1:2], scalar=1.0 / 2048, in1=st4[:, 3:4],
                                                             op0=ALU.mult, op1=ALU.subtract), reads=[st4], writes=[st4])
                c.op("act", lambda e: e.activation(out=st4[:, 5:6], in_=st4[:, 4:5], func=AF.Sqrt, bias=self.eps_t[:, 0:1]),
                     reads=[st4, self.eps_t], writes=[st4])
                c.op("dve", lambda e: e.reciprocal(out=st4[:, 6:7], in_=st4[:, 5:6]), reads=[st4], writes=[st4])
                c.op("dve", lambda e: e.tensor_scalar(out=vn[:, :], in0=vg[:, :], scalar1=st4[:, 2:3], scalar2=st4[:, 6:7],
                                                      op0=ALU.subtract, op1=ALU.mult), reads=[vg, st4], writes=[vn])
                c.op("pool", lambda e: e.tensor_tensor(out=vn[:, :], in0=vn[:, :], in1=lng[:, :], op=ALU.mult), reads=[vn, lng], writes=[vn])
                c.op("pool", lambda e: e.tensor_tensor(out=vbf[:, :], in0=vn[:, :], in1=lnb[:, :], op=ALU.add), reads=[vn, lnb], writes=[vbf])
                c.dma("sp", self.V[c0 + tb * 128:c0 + (tb + 1) * 128, :], vbf[:, :], reads=[vbf])

            tl = tiles_all()
            self.load_modnorm(S, hsrc, tl[0][0], W, 0, 0, aTs[0])
            for ti, (c0, _) in enumerate(tl):
                aT = aTs[ti % 2]; cur[0] = aT
                pend_norm = None
                if ti + 1 < len(tl):
                    pend_norm = self.gen_modnorm(S, hsrc, tl[ti + 1][0], W, 0, 0, aTs[(ti + 1) % 2])
                sb = subs(W)
                self.load_w(wbs[0], wi[:, ftiles[0][0]:ftiles[0][0] + 256], 16, 256, cache=self.ABIB[:, ftiles[0][0]:ftiles[0][0] + 256], key=("abi", 0))
                for t, (col0, kind, dst, row0) in enumerate(ftiles):
                    if t + 1 < len(ftiles):
                        nc0 = ftiles[t + 1][0]
                        self.load_w(wbs[(t + 1) % 2], wi[:, nc0:nc0 + 256], 16, 256, cache=self.ABIB[:, nc0:nc0 + 256], key=("abi", t + 1))
                    wb = wbs[t % 2]
                    for oi in range(2):
                        o = (obf if kind == "x" else ob)[st_ei[1] % 2]; st_ei[1] += 1
                        for (lo, w) in sb:
                            if pend_norm is not None and t >= 4:
                                next(pend_norm, None)
                            p = pg[st_ei[0] % 2]; st_ei[0] += 1
                            for kc in range(16):
                                c.op("pe", lambda e: e.matmul(p[:, 0:w], wb[:, kc, oi * 128:(oi + 1) * 128], aT[:, kc, lo:lo + w],
                                                              start=(kc == 0), stop=(kc == 15)),
                                     reads=[wb, aT], writes=[p], inc=(kc == 15))
                            fn = {"u": AF.Gelu, "z": AF.Silu, "x": AF.Copy}[kind]
                            c.op("act", lambda e: e.activation(out=o[:, lo:lo + w], in_=p[:, 0:w], func=fn), reads=[p], writes=[o], join=True)
                        r0 = row0 + oi * 128
                        c.dma("sp", dst[r0:r0 + 128, c0:c0 + W], o[:, 0:W], reads=[o])
                    if t % 4 == 3 and t // 4 < 6:
                        vblock(t // 4, c0)
                if pend_norm is not None:
                    for _ in pend_norm:
                        pass
                c.dma("sp", self.DTR[c0:c0 + W, :].rearrange("(n p) d -> p n d", p=128), dto[:, :, :], reads=[dto])

    def load_w_cols(self, wb, col0, src2d, KC, ncols, first=True):
        c = self.c
        for k0 in range(0, KC, 4):
            c.dma("pool", wb[:, k0:k0 + 4, col0:col0 + ncols],
                  src2d[k0 * 128:(k0 + 4) * 128, :].rearrange("(kc p) n -> p kc n", p=128),
                  writes=[wb], join=not (first and k0 == 0))

    def phase_gmlp(self):
        c, I = self.c, self.I
        with ExitStack() as st:
            c.stack = st
            wsT = c.sb([128, 8, 128], BF16, "wsT")
            c.dma("pool", wsT[:], I["wsT"][:, :, :], writes=[wsT])
            Bf = c.sb([128, 16, 128], F32, "Bf")
            c.dma("sp", Bf[:].rearrange("p a b -> p (a b)"), I["bs_row"][0:1, :].partition_broadcast(128), writes=[Bf])
            Ut = [c.sb([128, 16, 512], BF16, "Ut") for _ in range(2)]
            Vc = [c.sb([128, 2048], BF16, "Vc") for _ in range(2)]
            yo = [c.sb([128, 16, 512], BF16, "yo") for _ in range(2)]
            tf = [c.sb([128, 4, 128], F32, "tf") for _ in range(2)]
            pp = [c.ps() for _ in range(8)]
            vi = 0
            ti = 0
            for gi in range(NT // 512):
                g0 = gi * 512
                ut = Ut[gi % 2]; y = yo[gi % 2]
                for k0 in range(0, 16, 4):
                    c.dma("sp", ut[:, k0:k0 + 4, :], self.U[k0 * 128:(k0 + 4) * 128, g0:g0 + 512].rearrange("(kc p) n -> p kc n", p=128),
                          writes=[ut], join=(k0 > 0))
                for j in range(4):
                    vc = Vc[vi % 2]
                    c.dma("sp", vc[:, :], self.V[g0 + j * 128:g0 + (j + 1) * 128, :], writes=[vc])
                    for blk in range(16):
                        p = pp[(vi % 2) * 4 + blk // 4]
                        c.op("pe", lambda e: e.matmul(p[:, (blk % 4) * 128:(blk % 4 + 1) * 128], vc[:, blk * 128:(blk + 1) * 128], wsT[:, blk // 2, :],
                                                      start=True, stop=True),
                             reads=[vc, wsT], writes=[p], inc=(blk % 4 == 3))
                    for b in range(4):
                        p = pp[(vi % 2) * 4 + b]
                        t = tf[ti % 2]; ti += 1
                        c.op("dve", lambda e: e.tensor_tensor(out=t[:, :, :], in0=p[:, 0:512].rearrange("p (a b) -> p a b", b=128),
                                                              in1=Bf[:, 4 * b:4 * b + 4, :], op=ALU.add), reads=[p, Bf], writes=[t])
                        c.op("pool", lambda e: e.tensor_tensor(out=y[:, 4 * b:4 * b + 4, j * 128:(j + 1) * 128], in0=t[:, :, :],
                                                               in1=ut[:, 4 * b:4 * b + 4, j * 128:(j + 1) * 128], op=ALU.mult),
                             reads=[t, ut], writes=[y], join=True)
                    vi += 1
                for k0 in range(0, 16, 4):
                    c.dma("sp", self.YAB[k0 * 128:(k0 + 4) * 128, g0:g0 + 512].rearrange("(kc p) n -> p kc n", p=128), y[:, k0:k0 + 4, :], reads=[y])

    def phase_conv(self):
        c, I = self.c, self.I
        with ExitStack() as st:
            c.stack = st
            cw = c.sb([128, 24, 5], F32, "cw"); cb = c.sb([128, 24], F32, "cb")
            c.dma("sp", cw[:], I["conv_w"][:, :, :], writes=[cw]); c.dma("sp", cb[:], I["conv_b"][:, :], writes=[cb])
            xb = [c.sb([128, SEQT], F32, "xb") for _ in range(3)]
            acc = [c.sb([128, SEQT], F32, "acc") for _ in range(3)]
            it = 0
            for s in range(2):
                for cc in range(24):
                    x = xb[it % 3]; a = acc[it % 3]; it += 1
                    c.dma("sp", x[:, :], self.XBC[cc * 128:(cc + 1) * 128, s * SEQT:(s + 1) * SEQT], writes=[x])
                    c.op("act", lambda e: e.activation(out=a[:, :], in_=x[:, :], func=AF.Identity, scale=cw[:, cc, 2:3], bias=cb[:, cc:cc + 1]),
                         reads=[x, cw, cb], writes=[a])
                    for k in (0, 1, 3, 4):
                        dlt = k - 2
                        for (sa, sbb) in ((0, CTXL), (CTXL, SEQT)):
                            lo = max(sa, sa - dlt); hi = min(sbb, sbb - dlt)
                            c.op("dve", lambda e: e.scalar_tensor_tensor(out=a[:, lo:hi], in0=x[:, lo + dlt:hi + dlt], scalar=cw[:, cc, k:k + 1],
                                                                         in1=a[:, lo:hi], op0=ALU.mult, op1=ALU.add),
                                 reads=[x, cw, a], writes=[a])
                    c.op("act", lambda e: e.activation(out=x[:, :], in_=a[:, :], func=AF.Silu), reads=[a], writes=[x])
                    c.dma("sp", self.XC[cc * 128:(cc + 1) * 128, s * SEQT:(s + 1) * SEQT], x[:, :], reads=[x])

    def phase_ssd(self):
        c, I = self.c, self.I
        with ExitStack() as st:
            c.stack = st
            tri = [c.sb([128, 128], F32, "tri") for _ in range(2)]
            ident = c.sb([128, 128], F32, "ident")
            c.dma("sp", tri[0][:], I["tri_f"][:, :], writes=[tri[0]]); c.dma("sp", tri[1][:], I["tri_b"][:, :], writes=[tri[1]])
            c.dma("sp", ident[:], I["ident"][:, :], writes=[ident])
            a_b = c.sb([128, 64], F32, "a_b"); dtb = c.sb([128, 64], F32, "dtb")
            c.dma("sp", a_b[:], I["alog_row"][0:1, :].partition_broadcast(128), writes=[a_b])
            c.dma("sp", dtb[:], I["dtb_row"][0:1, :].partition_broadcast(128), writes=[dtb])
            c.op("act", lambda e: e.activation(out=a_b[:], in_=a_b[:], func=AF.Exp), reads=[a_b], writes=[a_b])
            c.op("dve", lambda e: e.tensor_scalar(out=a_b[:], in0=a_b[:], scalar1=-1.0, scalar2=None, op0=ALU.mult), reads=[a_b], writes=[a_b])
            bd = c.sb([128, 16], F32, "bd"); gng = c.sb([128, 16], F32, "gng")
            c.dma("sp", bd[:], I["bd"][:, :], writes=[bd]); c.dma("sp", gng[:], I["gng"][:, :], writes=[gng])
            dt = c.sb([128, 18, 64], F32, "dt"); dta = c.sb([128, 18, 64], F32, "dta")
            hT = c.sb([128, 2048], F32, "hT"); hTb = c.sb([128, 2048], BF16, "hTb")

            def slot():
                d = {}
                d["xs"] = c.sb([128, 16, 128], F32, "xsT")
                d["b"] = c.sb([128, 4, 128], F32, "bT"); d["c"] = c.sb([128, 4, 128], F32, "cT")
                d["bb"] = c.sb([128, 4, 128], BF16, "bTb"); d["cb"] = c.sb([128, 4, 128], BF16, "cTb")
                d["yf"] = c.sb([128, 16, 128], F32, "yfl"); d["sz"] = c.sb([128, 16, 128], BF16, "szl")
                d["cum"] = c.sb([128, 32], F32, "cum"); d["te"] = c.sb([128, 32], F32, "te"); d["cd"] = c.sb([128, 32], F32, "cd")
                d["tm"] = c.sb([128, 32], F32, "tm32")
                d["xdt"] = c.sb([128, 2048], F32, "xdt"); d["xdtb"] = c.sb([128, 2048], BF16, "xdtb"); d["xteb"] = c.sb([128, 2048], BF16, "xteb")
                d["btok"] = c.sb([128, 512], BF16, "btok"); d["cbm"] = c.sb([128, 4, 128], F32, "cbm")
                d["yo"] = c.sb([128, 16, 128], F32, "yo")
                return d
            SL = [slot(), slot()]
            sg = [c.sb([128, 4, 128], F32, "sg") for _ in range(2)]
            Ee = [c.sb([128, 4, 128], F32, "Ee") for _ in range(2)]
            ecb = [c.sb([128, 4, 128], F32, "ecb") for _ in range(2)]
            MT = [c.sb([128, 4, 128], BF16, "MT") for _ in range(2)]
            crT = [c.sb([128, 4, 128], BF16, "crT") for _ in range(2)]
            y2 = c.sb([128, 16, 128], F32, "y2"); sqy = c.sb([128, 16, 128], BF16, "sqy")
            rsn = c.sb([128, 4, 128], F32, "rsn"); ybf = c.sb([128, 16, 128], BF16, "ybf")
            pY = [c.ps() for _ in range(4)]
            pE = [c.ps() for _ in range(2)]
            pM = [c.ps() for _ in range(2)]
            st_ = {"mi": 0}

            def nextM():
                p = pM[st_["mi"] % 2]; st_["mi"] += 1
                return p

            def prep(s, di, n, d):
                T = tri[di]; h0 = di * 32
                col = s * SEQT + n * 128
                xs, b_, c_ = d["xs"], d["b"], d["c"]
                for k0 in range(0, 16, 4):
                    c.dma("sp", xs[:, k0:k0 + 4, :], self.XC[k0 * 128:(k0 + 4) * 128, col:col + 128].rearrange("(kc p) n -> p kc n", p=128),
                          writes=[xs], join=(k0 > 0))
                c.dma("sp", b_[:, :, :], self.XC[2048:2560, col:col + 128].rearrange("(kc p) n -> p kc n", p=128), writes=[b_])
                c.dma("sp", c_[:, :, :], self.XC[2560:3072, col:col + 128].rearrange("(kc p) n -> p kc n", p=128), writes=[c_])
                if di == 1:
                    for k0 in range(0, 16, 4):
                        c.dma("sp", d["yf"][:, k0:k0 + 4, :], self.YF[k0 * 128:(k0 + 4) * 128, col:col + 128].rearrange("(kc p) n -> p kc n", p=128),
                              writes=[d["yf"]], join=(k0 > 0))
                        c.dma("sp", d["sz"][:, k0:k0 + 4, :], self.SZ[k0 * 128:(k0 + 4) * 128, col:col + 128].rearrange("(kc p) n -> p kc n", p=128),
                              writes=[d["sz"]], join=(k0 > 0))
                c.op("act", lambda e: e.activation(out=d["bb"][:, :, :], in_=b_[:, :, :], func=AF.Copy), reads=[b_], writes=[d["bb"]])
                c.op("act", lambda e: e.activation(out=d["cb"][:, :, :], in_=c_[:, :, :], func=AF.Copy), reads=[c_], writes=[d["cb"]])
                yield
                dsl = dta[:, n, h0:h0 + 32]
                pa = nextM()
                c.op("pe", lambda e: e.matmul(pa[:, 0:32], T[:, :], dsl, start=True, stop=True), reads=[T, dta], writes=[pa], inc=False)
                c.op("pe", lambda e: e.matmul(pa[:, 32:64], self.ones_f[:, :], dsl, start=True, stop=True), reads=[self.ones_f, dta], writes=[pa])
                c.op("act", lambda e: e.activation(out=d["cum"][:, :], in_=pa[:, 0:32], func=AF.Copy), reads=[pa], writes=[d["cum"]])
                c.op("dve", lambda e: e.tensor_tensor(out=d["tm"][:, :], in0=pa[:, 32:64], in1=d["cum"][:, :], op=ALU.subtract), reads=[pa, d["cum"]], writes=[d["tm"]])
                c.op("act", lambda e: e.activation(out=d["te"][:, :], in_=d["tm"][:, :], func=AF.Exp), reads=[d["tm"]], writes=[d["te"]])
                c.op("act", lambda e: e.activation(out=d["cd"][:, :], in_=pa[:, 32:64], func=AF.Exp), reads=[pa], writes=[d["cd"]])
                for b in range(4):
                    yield
                    pt = nextM()
                    for q in range(4):
                        cc = b * 4 + q
                        c.op("pe", lambda e: e.transpose(pt[:, q * 128:(q + 1) * 128], xs[:, cc, :], ident[:, :]), reads=[xs, ident], writes=[pt], inc=(q == 3))
                    c.op("dve", lambda e: e.tensor_tensor(out=d["xdt"][:, b * 512:(b + 1) * 512].rearrange("p (a b) -> p a b", b=64),
                                                          in0=pt[:, 0:512].rearrange("p (a b) -> p a b", b=64),
                                                          in1=dt[:, n, h0 + b * 8:h0 + b * 8 + 8].unsqueeze(2).to_broadcast([128, 8, 64]), op=ALU.mult),
                         reads=[pt, dt], writes=[d["xdt"]], join=(b > 0))
                yield
                c.op("act", lambda e: e.activation(out=d["xdtb"][:, :], in_=d["xdt"][:, :], func=AF.Copy), reads=[d["xdt"]], writes=[d["xdtb"]])
                c.op("pool", lambda e: e.tensor_tensor(out=d["xteb"][:, :].rearrange("p (a b) -> p a b", b=64),
                                                       in0=d["xdt"][:, :].rearrange("p (a b) -> p a b", b=64),
                                                       in1=d["te"][:, :].unsqueeze(2).to_broadcast([128, 32, 64]), op=ALU.mult),
                     reads=[d["xdt"], d["te"]], writes=[d["xteb"]])
                yield
                pb = nextM()
                for g in range(4):
                    c.op("pe", lambda e: e.transpose(pb[:, g * 128:(g + 1) * 128], b_[:, g, :], ident[:, :]), reads=[b_, ident], writes=[pb], inc=(g == 3))
                c.op("act", lambda e: e.activation(out=d["btok"][:, :], in_=pb[:, 0:512], func=AF.Copy), reads=[pb], writes=[d["btok"]])
                yield
                pc = nextM()
                for g in range(4):
                    c.op("pe", lambda e: e.matmul(pc[:, g * 128:(g + 1) * 128], d["bb"][:, g, :], d["cb"][:, g, :], start=True, stop=True),
                         reads=[d["bb"], d["cb"]], writes=[pc], inc=(g == 3))
                c.op("dve", lambda e: e.tensor_tensor(out=d["cbm"][:, :, :], in0=pc[:, 0:512].rearrange("p (a b) -> p a b", b=128),
                                                      in1=T[:, :].unsqueeze(1).to_broadcast([128, 4, 128]), op=ALU.mult),
                     reads=[pc, T], writes=[d["cbm"]])
                yield
                if di == 1:
                    c.op("pool", lambda e: e.tensor_tensor(out=d["yo"][:, :, :], in0=xs[:, :, :], in1=bd[:, :].unsqueeze(2).to_broadcast([128, 16, 128]), op=ALU.mult),
                         reads=[xs, bd], writes=[d["yo"]])
                    c.op("pool", lambda e: e.tensor_tensor(out=d["yo"][:, :, :], in0=d["yo"][:, :, :], in1=d["yf"][:, :, :], op=ALU.add),
                         reads=[d["yo"], d["yf"]], writes=[d["yo"]])

            def main(s, di, n, d, pend):
                T = tri[di]; h0 = di * 32
                col = s * SEQT + n * 128
                c_ = d["c"]

                def cumb(t):
                    pe_ = pE[t % 2]
                    for r4 in range(4):
                        hh = t * 4 + r4
                        c.op("pe", lambda e: e.matmul(pe_[:, r4 * 128:(r4 + 1) * 128], dta[:, n, h0 + hh:h0 + hh + 1].to_broadcast([128, 128]), T[:, :],
                                                      start=True, stop=True), reads=[dta, T], writes=[pe_], inc=(r4 == 3))
                cumb(0)
                for t in range(8):
                    if t + 1 < 8:
                        cumb(t + 1)
                    k = t % 2; g = t // 2
                    pe_ = pE[k]
                    for r4 in range(4):
                        hh = t * 4 + r4
                        c.op("dve", lambda e: e.scalar_tensor_tensor(out=sg[k][:, r4, :], in0=pe_[:, r4 * 128:(r4 + 1) * 128], scalar=d["cum"][:, hh:hh + 1],
                                                                     in1=T[:, :], op0=ALU.subtract, op1=ALU.mult),
                             reads=[pe_, d["cum"], T], writes=[sg[k]], join=(r4 > 0))
                    c.op("act", lambda e: e.activation(out=Ee[k][:, :, :], in_=sg[k][:, :, :], func=AF.Exp), reads=[sg[k]], writes=[Ee[k]])
                    c.op("act", lambda e: e.activation(out=ecb[k][:, :, :], in_=pe_[:, 0:512].rearrange("p (a b) -> p a b", b=128), func=AF.Exp),
                         reads=[pe_], writes=[ecb[k]])
                    c.op("dve", lambda e: e.tensor_tensor(out=MT[k][:, :, :], in0=Ee[k][:, :, :], in1=d["cbm"][:, g:g + 1, :].to_broadcast([128, 4, 128]), op=ALU.mult),
                         reads=[Ee[k], d["cbm"]], writes=[MT[k]])
                    c.op("pool", lambda e: e.tensor_tensor(out=crT[k][:, :, :], in0=ecb[k][:, :, :], in1=c_[:, g:g + 1, :].to_broadcast([128, 4, 128]), op=ALU.mult),
                         reads=[ecb[k], c_], writes=[crT[k]])
                    for r4 in range(4):
                        hh = t * 4 + r4
                        cc = hh // 2; half = hh % 2
                        out = pY[cc // 4][64 * half:64 * half + 64, (cc % 4) * 128:(cc % 4 + 1) * 128]
                        c.op("pe", lambda e: e.matmul(out, d["xdtb"][:, hh * 64:(hh + 1) * 64], MT[k][:, r4, :], start=True, stop=False),
                             reads=[d["xdtb"], MT[k]], writes=[pY[cc // 4]], inc=False)
                        c.op("pe", lambda e: e.matmul(out, hTb[:, hh * 64:(hh + 1) * 64], crT[k][:, r4, :], start=False, stop=True),
                             reads=[hTb, crT[k]], writes=[pY[cc // 4]], inc=True)
                if pend is not None:
                    for _ in pend:
                        pass
                for g in range(4):
                    psn = nextM()
                    c.op("pe", lambda e: e.matmul(psn[:, 0:512], d["btok"][:, g * 128:(g + 1) * 128], d["xteb"][:, g * 512:(g + 1) * 512], start=True, stop=True),
                         reads=[d["btok"], d["xteb"]], writes=[psn])
                    hv = hT[:, g * 512:(g + 1) * 512]
                    c.op("dve", lambda e: e.tensor_tensor(out=hv.rearrange("p (a b) -> p a b", b=64), in0=hv.rearrange("p (a b) -> p a b", b=64),
                                                          in1=d["cd"][:, g * 8:(g + 1) * 8].unsqueeze(2).to_broadcast([128, 8, 64]), op=ALU.mult),
                         reads=[hT, d["cd"]], writes=[hT])
                    c.op("dve", lambda e: e.tensor_tensor(out=hv, in0=psn[:, 0:512], in1=hv, op=ALU.add), reads=[psn, hT], writes=[hT])
                c.op("act", lambda e: e.activation(out=hTb[:, :], in_=hT[:, :], func=AF.Copy), reads=[hT], writes=[hTb])
                if di == 0:
                    yout = d["yo"]
                    for b in range(4):
                        c.op("act", lambda e: e.activation(out=yout[:, 4 * b:4 * b + 4, :], in_=pY[b][:, 0:512].rearrange("p (a b) -> p a b", b=128), func=AF.Copy),
                             reads=[pY[b]], writes=[yout], join=(b > 0))
                    for k0 in range(0, 16, 4):
                        c.dma("sp", self.YF[k0 * 128:(k0 + 4) * 128, col:col + 128].rearrange("(kc p) n -> p kc n", p=128), yout[:, k0:k0 + 4, :], reads=[yout])
                else:
                    for b in range(4):
                        c.op("dve", lambda e: e.tensor_tensor(out=y2[:, 4 * b:4 * b + 4, :], in0=pY[b][:, 0:512].rearrange("p (a b) -> p a b", b=128),
                                                              in1=d["yo"][:, 4 * b:4 * b + 4, :], op=ALU.add), reads=[pY[b], d["yo"]], writes=[y2], join=(b > 0))
                    c.op("dve", lambda e: e.tensor_tensor(out=y2[:, :, :], in0=y2[:, :, :], in1=d["sz"][:, :, :], op=ALU.mult), reads=[y2, d["sz"]], writes=[y2])
                    c.op("act", lambda e: e.activation(out=sqy[:, :, :], in_=y2[:, :, :], func=AF.Square), reads=[y2], writes=[sqy])
                    pn = nextM()
                    for g in range(4):
                        for j in range(4):
                            c.op("pe", lambda e: e.matmul(pn[:, g * 128:(g + 1) * 128], self.ones_bf[:, :], sqy[:, g * 4 + j, :], start=(j == 0), stop=(j == 3)),
                                 reads=[self.ones_bf, sqy], writes=[pn], inc=(g == 3 and j == 3))
                    c.op("act", lambda e: e.activation(out=rsn[:, :, :], in_=pn[:, 0:512].rearrange("p (a b) -> p a b", b=128), func=AF.Sqrt,
                                                       scale=1.0 / 512, bias=self.eps_t[:, 0:1]), reads=[pn, self.eps_t], writes=[rsn])
                    c.op("dve", lambda e: e.reciprocal(out=rsn[:, :, :], in_=rsn[:, :, :]), reads=[rsn], writes=[rsn])
                    for cc in range(16):
                        c.op("dve", lambda e: e.scalar_tensor_tensor(out=ybf[:, cc, :], in0=y2[:, cc, :], scalar=gng[:, cc:cc + 1], in1=rsn[:, cc // 4, :],
                                                                     op0=ALU.mult, op1=ALU.mult),
                             reads=[y2, gng, rsn], writes=[ybf], join=(cc > 0))
                    for k0 in range(0, 16, 4):
                        c.dma("sp", self.YAB[2048 + k0 * 128:2048 + (k0 + 4) * 128, col:col + 128].rearrange("(kc p) n -> p kc n", p=128), ybf[:, k0:k0 + 4, :], reads=[ybf])

            gi = 0
            for s in range(2):
                c.dma("sp", dt[:, :, :], self.DTR[s * SEQT:(s + 1) * SEQT, :].rearrange("(n p) d -> p n d", p=128), writes=[dt])
                c.op("dve", lambda e: e.tensor_tensor(out=dt[:, :, :], in0=dt[:, :, :], in1=dtb[:, :].unsqueeze(1).to_broadcast([128, 18, 64]), op=ALU.add),
                     reads=[dt, dtb], writes=[dt])
                c.op("act", lambda e: e.activation(out=dt[:, :, :], in_=dt[:, :, :], func=AF.Exp), reads=[dt], writes=[dt])
                c.op("act", lambda e: e.activation(out=dt[:, :, :], in_=dt[:, :, :], func=AF.Ln, bias=self.ones_f[:, 0:1]), reads=[dt, self.ones_f], writes=[dt])
                c.op("dve", lambda e: e.tensor_tensor(out=dta[:, :, :], in0=dt[:, :, :], in1=a_b[:, :].unsqueeze(1).to_broadcast([128, 18, 64]), op=ALU.mult),
                     reads=[dt, a_b], writes=[dta])
                for di in range(2):
                    order = [0, 1] + list(range(2, 18)) if di == 0 else [1, 0] + list(range(17, 1, -1))
                    c.op("dve", lambda e: e.memset(hT[:, :], 0.0), writes=[hT])
                    c.op("dve", lambda e: e.memset(hTb[:, :], 0.0), writes=[hTb])
                    for _ in prep(s, di, order[0], SL[gi % 2]):
                        pass
                    for idx, n in enumerate(order):
                        pend = prep(s, di, order[idx + 1], SL[(gi + 1) % 2]) if idx + 1 < len(order) else None
                        main(s, di, n, SL[gi % 2], pend)
                        gi += 1

    def layer0(self, h0):
        I = self.I
        self.phase_ab_inproj(h0)
        if self.end_phase("ab_in"):
            return
        self.phase_gmlp()
        if self.end_phase("gmlp"):
            return
        self.phase_conv()
        if self.end_phase("conv"):
            return
        self.phase_ssd()
        if self.end_phase("ssd"):
            return
        self.phase_outproj(0, I["ab_w_out"], 32, h0, self.HA, tiles_all())
        if self.end_phase("ab_out"):
            return
        self.phase_mlp(0, self.HA, self.HB, tiles_all(), lambda c0: c0)
        if self.end_phase("mlp0"):
            return


def _consts():
    ident = np.eye(128, dtype=np.float32)
    k = np.arange(128)
    tri_f = (k[:, None] <= k[None, :]).astype(np.float32)
    tri_b = (k[:, None] >= k[None, :]).astype(np.float32)
    R = np.zeros((128, 128), np.float32)
    for d in range(128):
        half = (d % 64) // 32
        if half == 0:
            R[d, d + 32] = -1.0
        else:
            R[d, d - 32] = 1.0
    rotT = np.ascontiguousarray(R.T)
    t = np.arange(LAT)
    pos = np.stack([t // 64, t % 64], axis=-1).astype(np.float32)
    inv_freq = (10000.0 ** (-np.arange(0, 64, 2, dtype=np.float32) / 64)).astype(np.float32)
    ang = pos[:, :, None] * inv_freq
    cosE = np.ones((128, SEQT), np.float32); sinE = np.zeros((128, SEQT), np.float32)
    for d in range(128):
        ax = d // 64; f = d % 32
        cosE[d, CTXL:] = np.cos(ang[:, ax, f]); sinE[d, CTXL:] = np.sin(ang[:, ax, f])
    wmask = np.ones((128, 3, 128), np.float32)
    wmask[:, 0, :] = (k[:, None] >= k[None, :])
    wmask[:, 2, :] = (k[:, None] <= k[None, :])
    return dict(ident=ident, tri_f=tri_f, tri_b=tri_b, rotT=rotT, cosE=cosE, sinE=sinE, wmask=wmask)


def _chunked(v):
    return np.ascontiguousarray(v.reshape(-1, 128).T)


def make_in_maps(x, c, ctx, c_ctx, mod_w, mod_b, norm_g, mlp_w1, mlp_w2, ab_w_in, a_w_s, a_b_s,
                 a_ln_g, a_ln_b, b_conv_w, b_conv_b, b_a_log, b_dt_bias, b_d, b_norm_g, ab_w_out,
                 cd_w_in, c_q_norm_g, c_k_norm_g, d_sink, cd_w_out):
    f = lambda a: np.ascontiguousarray(np.asarray(a, dtype=np.float32))
    shared = dict(_consts())
    shared["mod_w"] = f(mod_w)
    shared["mod_b"] = np.ascontiguousarray(f(mod_b).reshape(2, 96, 128).transpose(2, 0, 1))
    shared["norm_g"] = np.ascontiguousarray(f(norm_g).reshape(2, 4, 16, 128).transpose(3, 0, 1, 2))
    shared["mlp_w1"] = f(mlp_w1); shared["mlp_w2"] = f(mlp_w2)
    shared["ab_w_in"] = f(ab_w_in[0]); shared["ab_w_out"] = f(ab_w_out[0])
    shared["cd_w_in"] = f(cd_w_in[0]); shared["cd_w_out"] = f(cd_w_out[0])
    shared["wsT"] = np.ascontiguousarray(f(a_w_s[0]).transpose(2, 0, 1))
    shared["bs_row"] = np.ascontiguousarray(np.repeat(f(a_b_s[0]), 2, axis=0).reshape(1, 2048))
    shared["lng_row"] = f(a_ln_g[0]).reshape(1, 2048); shared["lnb_row"] = f(a_ln_b[0]).reshape(1, 2048)
    shared["conv_w"] = np.ascontiguousarray(f(b_conv_w[0]).reshape(5, 24, 128).transpose(2, 1, 0))
    shared["conv_b"] = np.ascontiguousarray(f(b_conv_b[0]).reshape(24, 128).T)
    shared["alog_row"] = f(b_a_log[0]).reshape(1, 64); shared["dtb_row"] = f(b_dt_bias[0]).reshape(1, 64)
    shared["bd"] = _chunked(np.repeat(f(b_d[0]), 64)); shared["gng"] = _chunked(f(b_norm_g[0]))
    shared["qg"] = f(c_q_norm_g[0]).reshape(128, 1); shared["kg"] = f(c_k_norm_g[0]).reshape(128, 1)
    shared["sink_row"] = f(d_sink[0]).reshape(1, 8)
    x = np.asarray(x); ctx = np.asarray(ctx); c = f(c); c_ctx = f(c_ctx)
    maps = []
    for core in range(NCORES):
        b0 = 2 * core
        hT = np.empty((D, NT), np.float32)
        for s in range(2):
            hT[:, s * SEQT:s * SEQT + CTXL] = ctx[b0 + s].T
            hT[:, s * SEQT + CTXL:(s + 1) * SEQT] = x[b0 + s].T
        cT = np.stack([c[b0], c[b0 + 1], c_ctx], axis=0)
        cT = np.ascontiguousarray(cT.reshape(3, 16, 128).transpose(2, 1, 0))
        m = dict(shared)
        m["hT"] = hT; m["cT"] = cT
        maps.append(m)
    return maps


_PROG_CACHE = {}


def get_prog(**kw):
    key = repr(sorted(kw.items()))
    if key not in _PROG_CACHE:
        p = Prog(**kw)
        p.build()
        _PROG_CACHE[key] = p
    return _PROG_CACHE[key]


def kernel(**inputs):
    p = get_prog()
    maps = make_in_maps(**inputs)
    res = run_bass_kernel_spmd(p.nc, maps, core_ids=list(range(NCORES)))
    out = np.empty((16, LAT, D), np.float32)
    for core in range(NCORES):
        oT = res.results[core]["outT"]
        for s in range(2):
            out[2 * core + s] = oT[:, s * LAT:(s + 1) * LAT].T
    return out
```
